# Optimizing a Trainium2 kernel written in Bass

```python
import jax, jax.numpy as jnp
from jax import lax
import numpy as np

D_MODEL = 2048
BATCH = 8
SEQ = 2048
DEPTH = 1

CHUNK = 64
Q_BLOCK = 128
EPS = 1e-6

MLA_HEADS = 8
Q_LORA = 768
KV_LORA = 512
QK_NOPE = 128
QK_ROPE = 64
V_HEAD = 128
ROPE_THETA = 10000.0
MLA_WIDTH = 1024

SSM_HEADS = 16
SSM_HEAD_DIM = 64
SSM_INNER = 1024
SSM_GROUPS = 2
SSM_STATE = 128
SSM_CONV = 4
SSM_CONV_CH = 1536

MIX_WIDTH = 2048
IN_COLS = 3920

D_FF = 5632
FFN_CONV = 3

kernel_name = "hymba_mla_ssd_convffn_sandwich"


def rms_norm(x, g):
    xf = x.astype(jnp.float32)
    y = xf * lax.rsqrt(jnp.mean(xf * xf, axis=-1, keepdims=True) + EPS)
    return (y * g.astype(jnp.float32)).astype(x.dtype)


def causal_dwconv(x, w, b):
    K, C = w.shape
    y = lax.conv_general_dilated(
        x, w[:, None, :].astype(x.dtype), window_strides=(1,), padding=[(K - 1, 0)],
        dimension_numbers=('NWC', 'WIO', 'NWC'), feature_group_count=C)
    return y + b.astype(x.dtype)


def rope_tables(S):
    inv = 1.0 / (ROPE_THETA ** (jnp.arange(0, QK_ROPE, 2, dtype=jnp.float32) / QK_ROPE))
    ang = jnp.arange(S, dtype=jnp.float32)[:, None] * inv[None, :]
    return jnp.cos(ang), jnp.sin(ang)


def rotate(x, cos, sin):
    x1, x2 = jnp.split(x, 2, axis=-1)
    c = cos.astype(x.dtype)
    s = sin.astype(x.dtype)
    return jnp.concatenate([x1 * c - x2 * s, x1 * s + x2 * c], axis=-1)


def mla_mixer(c_q, c_kv, k_rope, q_norm_g, w_uq, kv_norm_g, w_ukv):
    Bsz, S, _ = c_q.shape
    q = (rms_norm(c_q, q_norm_g) @ w_uq).reshape(Bsz, S, MLA_HEADS, QK_NOPE + QK_ROPE)
    q_nope, q_rope = q[..., :QK_NOPE], q[..., QK_NOPE:]
    kv = (rms_norm(c_kv, kv_norm_g) @ w_ukv).reshape(Bsz, S, MLA_HEADS, QK_NOPE + V_HEAD)
    k_nope, v = kv[..., :QK_NOPE], kv[..., QK_NOPE:]
    cos, sin = rope_tables(S)
    q_rope = rotate(q_rope, cos[None, :, None, :], sin[None, :, None, :])
    k_rope = rotate(k_rope, cos[None], sin[None])
    scale = (QK_NOPE + QK_ROPE) ** -0.5
    nblk = S // Q_BLOCK
    key_chunk = jnp.arange(S) // CHUNK
    qn_b = q_nope.reshape(Bsz, nblk, Q_BLOCK, MLA_HEADS, QK_NOPE).transpose(1, 0, 2, 3, 4)
    qr_b = q_rope.reshape(Bsz, nblk, Q_BLOCK, MLA_HEADS, QK_ROPE).transpose(1, 0, 2, 3, 4)

    def block(args):
        qn, qr, i = args
        s = (jnp.einsum('bqhd,bkhd->bhqk', qn, k_nope)
             + jnp.einsum('bqhr,bkr->bhqk', qr, k_rope)).astype(jnp.float32) * scale
        q_chunk = (i * Q_BLOCK + jnp.arange(Q_BLOCK)) // CHUNK
        mask = key_chunk[None, :] <= q_chunk[:, None]
        s = jnp.where(mask, s, -1e30)
        p = jax.nn.softmax(s, axis=-1).astype(v.dtype)
        return jnp.einsum('bhqk,bkhd->bqhd', p, v)

    o = lax.map(block, (qn_b, qr_b, jnp.arange(nblk)))
    return o.transpose(1, 0, 2, 3, 4).reshape(Bsz, S, MLA_WIDTH)


def segsum(a):
    T = a.shape[-1]
    cs = jnp.cumsum(a, axis=-1)
    d = cs[..., :, None] - cs[..., None, :]
    return jnp.where(jnp.tril(jnp.ones((T, T), dtype=bool)), d, -jnp.inf)


def ssd_chunked(x, a, b, c):
    Bsz, S, H, P = x.shape
    N = b.shape[-1]
    nc = S // CHUNK
    x = x.reshape(Bsz, nc, CHUNK, H, P)
    b = b.reshape(Bsz, nc, CHUNK, H, N)
    c = c.reshape(Bsz, nc, CHUNK, H, N)
    a = a.reshape(Bsz, nc, CHUNK, H).transpose(0, 3, 1, 2)
    a_cs = jnp.cumsum(a, axis=-1)
    L = jnp.exp(segsum(a))
    y_diag = jnp.einsum('bclhn,bcshn,bhcls,bcshp->bclhp', c, b, L, x)
    decay_states = jnp.exp(a_cs[..., -1:] - a_cs)
    states = jnp.einsum('bclhn,bhcl,bclhp->bchpn', b, decay_states, x)
    chunk_decay = jnp.exp(a_cs[..., -1])

    def step(h, inp):
        st, dec = inp
        return h * dec[..., None, None] + st, h

    h0 = jnp.zeros((Bsz, H, P, N), x.dtype)
    _, states_in = lax.scan(step, h0, (states.transpose(1, 0, 2, 3, 4), chunk_decay.transpose(2, 0, 1)))
    states_in = states_in.transpose(1, 0, 2, 3, 4)
    y_off = jnp.einsum('bclhn,bchpn,bhcl->bclhp', c, states_in, jnp.exp(a_cs))
    return (y_diag + y_off).reshape(Bsz, S, H, P)


def ssd_mixer(z, xbc, dt, conv_w, conv_b, dt_bias, a_log, d_skip, norm_g):
    Bsz, S, _ = z.shape
    f32 = jnp.float32
    xbc = jax.nn.silu(causal_dwconv(xbc, conv_w, conv_b))
    gn = SSM_GROUPS * SSM_STATE
    xs = xbc[..., :SSM_INNER].reshape(Bsz, S, SSM_HEADS, SSM_HEAD_DIM).astype(f32)
    bm = xbc[..., SSM_INNER:SSM_INNER + gn].reshape(Bsz, S, SSM_GROUPS, SSM_STATE)
    cm = xbc[..., SSM_INNER + gn:].reshape(Bsz, S, SSM_GROUPS, SSM_STATE)
    rep = SSM_HEADS // SSM_GROUPS
    bh = jnp.repeat(bm, rep, axis=2).astype(f32)
    ch = jnp.repeat(cm, rep, axis=2).astype(f32)
    dtf = jax.nn.softplus(dt.astype(f32) + dt_bias.astype(f32))
    A = -jnp.exp(a_log.astype(f32))
    y = ssd_chunked(xs * dtf[..., None], dtf * A, bh, ch)
    y = y + xs * d_skip.astype(f32)[:, None]
    y = y.reshape(Bsz, S, SSM_INNER) * jax.nn.silu(z.astype(f32))
    yg = y.reshape(Bsz, S, SSM_GROUPS, SSM_INNER // SSM_GROUPS)
    yg = yg * lax.rsqrt(jnp.mean(yg * yg, axis=-1, keepdims=True) + EPS)
    y = yg.reshape(Bsz, S, SSM_INNER) * norm_g.astype(f32)
    return y.astype(z.dtype)


def conv_glu_ffn(h, w_gate, w_up, conv_w, conv_b, w_down):
    g = causal_dwconv(h @ w_gate, conv_w, conv_b)
    return (jax.nn.gelu(g, approximate=True) * (h @ w_up)) @ w_down


def hybrid_layer(x, mix_pre_g, w_in, q_norm_g, w_uq, kv_norm_g, w_ukv, ssm_conv_w, ssm_conv_b,
                 dt_bias, a_log, d_skip, ssm_norm_g, w_out, mix_post_g, ffn_pre_g, w_gate, w_up,
                 ffn_conv_w, ffn_conv_b, w_down, ffn_post_g):
    u = rms_norm(x, mix_pre_g) @ w_in
    cuts = np.cumsum([Q_LORA, KV_LORA, QK_ROPE, SSM_INNER, SSM_CONV_CH]).tolist()
    c_q, c_kv, k_rope, z, xbc, dt = jnp.split(u, cuts, axis=-1)
    a_out = mla_mixer(c_q, c_kv, k_rope, q_norm_g, w_uq, kv_norm_g, w_ukv)
    b_out = ssd_mixer(z, xbc, dt, ssm_conv_w, ssm_conv_b, dt_bias, a_log, d_skip, ssm_norm_g)
    mix = jnp.concatenate([a_out, b_out], axis=-1) @ w_out
    x = x + rms_norm(mix, mix_post_g)
    f = conv_glu_ffn(rms_norm(x, ffn_pre_g), w_gate, w_up, ffn_conv_w, ffn_conv_b, w_down)
    return x + rms_norm(f, ffn_post_g)


def setup_inputs(seed: int = 0) -> dict:
    key = jax.random.key(seed)
    ks = jax.random.split(key, 24)
    f32 = jnp.float32

    def nrm(k, shape, fan_in):
        return jax.random.normal(k, shape, f32) * (fan_in ** -0.5)

    def gain(k, n):
        return 1.0 + 0.02 * jax.random.normal(k, (DEPTH, n), f32)

    dt0 = jnp.exp(jax.random.uniform(ks[9], (DEPTH, SSM_HEADS), f32, np.log(1e-3), np.log(1e-1)))
    return {
        "x": jax.random.normal(ks[0], (BATCH, SEQ, D_MODEL), f32),
        "mix_pre_g": gain(ks[1], D_MODEL),
        "w_in": nrm(ks[2], (DEPTH, D_MODEL, IN_COLS), D_MODEL),
        "q_norm_g": gain(ks[3], Q_LORA),
        "w_uq": nrm(ks[4], (DEPTH, Q_LORA, MLA_HEADS * (QK_NOPE + QK_ROPE)), Q_LORA),
        "kv_norm_g": gain(ks[5], KV_LORA),
        "w_ukv": nrm(ks[6], (DEPTH, KV_LORA, MLA_HEADS * (QK_NOPE + V_HEAD)), KV_LORA),
        "ssm_conv_w": nrm(ks[7], (DEPTH, SSM_CONV, SSM_CONV_CH), SSM_CONV),
        "ssm_conv_b": 0.01 * jax.random.normal(ks[8], (DEPTH, SSM_CONV_CH), f32),
        "dt_bias": dt0 + jnp.log(-jnp.expm1(-dt0)),
        "a_log": jnp.log(jax.random.uniform(ks[10], (DEPTH, SSM_HEADS), f32, 1.0, 16.0)),
        "d_skip": gain(ks[11], SSM_HEADS),
        "ssm_norm_g": gain(ks[12], SSM_INNER),
        "w_out": nrm(ks[13], (DEPTH, MIX_WIDTH, D_MODEL), MIX_WIDTH),
        "mix_post_g": gain(ks[14], D_MODEL),
        "ffn_pre_g": gain(ks[15], D_MODEL),
        "w_gate": nrm(ks[16], (DEPTH, D_MODEL, D_FF), D_MODEL),
        "w_up": nrm(ks[17], (DEPTH, D_MODEL, D_FF), D_MODEL),
        "ffn_conv_w": nrm(ks[18], (DEPTH, FFN_CONV, D_FF), FFN_CONV),
        "ffn_conv_b": 0.01 * jax.random.normal(ks[19], (DEPTH, D_FF), f32),
        "w_down": nrm(ks[20], (DEPTH, D_FF, D_MODEL), D_FF),
        "ffn_post_g": gain(ks[21], D_MODEL),
    }


def reference(x, mix_pre_g, w_in, q_norm_g, w_uq, kv_norm_g, w_ukv, ssm_conv_w, ssm_conv_b,
              dt_bias, a_log, d_skip, ssm_norm_g, w_out, mix_post_g, ffn_pre_g, w_gate, w_up,
              ffn_conv_w, ffn_conv_b, w_down, ffn_post_g):
    for l in range(DEPTH):
        x = hybrid_layer(x, mix_pre_g[l], w_in[l], q_norm_g[l], w_uq[l], kv_norm_g[l], w_ukv[l],
                         ssm_conv_w[l], ssm_conv_b[l], dt_bias[l], a_log[l], d_skip[l], ssm_norm_g[l],
                         w_out[l], mix_post_g[l], ffn_pre_g[l], w_gate[l], w_up[l],
                         ffn_conv_w[l], ffn_conv_b[l], w_down[l], ffn_post_g[l])
    return x
```

```python
import numpy as np
import ml_dtypes
import concourse.bass as bass
import concourse.mybir as mybir
from concourse.bass_utils import run_bass_kernel_spmd

F32 = mybir.dt.float32
BF16 = mybir.dt.bfloat16
AF = mybir.ActivationFunctionType
ALU = mybir.AluOpType

S = 2048
D = 2048
NT = 16
KC = 16
EPS = 1e-6
DFF = 5632
NJ = 44
SB_BASE = 16512
SB_END = 229376
WIN_COLS = 4112
SCALE = 192.0 ** -0.5


class Buf:
    __slots__ = ("name", "w", "r", "sem", "rng", "excl")

    def __init__(self, name):
        self.name = name
        self.w = {}
        self.r = {}
        self.sem = None
        self.rng = None
        self.excl = False


class Prog:
    CE = ("pe", "act", "dve", "pool")

    def __init__(self, nc):
        self.nc = nc
        self.stream = {e: [] for e in ("pe", "act", "dve", "pool", "sp")}
        self.tick = {e: 0 for e in self.CE}
        self.pending = {e: False for e in self.CE}
        self.seen = {e: {} for e in self.stream}
        self.semval = {}
        self.sems = {}
        self.regions = []
        self.off = SB_BASE
        self.nbuf = 0

    def sem_for(self, key):
        if key not in self.sems:
            self.sems[key] = self.nc.alloc_semaphore(name="s_" + key)
            self.semval[key] = 0
        return self.sems[key]

    def newbuf(self, name, rng=None, dma=False, sem=None):
        b = Buf(name)
        self.nbuf += 1
        if sem is not None:
            b.sem = sem
        elif dma:
            b.sem = "d%d_%s" % (self.nbuf, name)
            self.sem_for(b.sem)
        if rng is not None:
            b.rng = rng
            for (o, e, ob) in self.regions:
                if o < rng[1] and rng[0] < e:
                    for k, v in ob.w.items():
                        b.r[k] = max(b.r.get(k, 0), v)
                    for k, v in ob.r.items():
                        b.r[k] = max(b.r.get(k, 0), v)
            self.regions.append((rng[0], rng[1], b))
        return b

    def alloc(self, name, shape, dtype, off, dma=False):
        nbytes = int(np.prod(shape[1:])) * (2 if dtype == BF16 else 4)
        assert off >= SB_BASE and off + nbytes <= SB_END, (name, off, nbytes)
        t = self.nc.alloc_sbuf_tensor_at(name, list(shape), dtype, offset=off)
        b = self.newbuf(name, (off, off + nbytes), dma=dma)
        return t, b

    def sub(self, parent, name, dma=False):
        return self.newbuf(name, parent.rng, dma=dma)

    def rebuf(self, old):
        return self.newbuf(old.name, old.rng, sem=old.sem)

    def _deps(self, eng, reads, writes):
        need = {}
        raw = set()
        for b in reads:
            for k, v in b.w.items():
                need[k] = max(need.get(k, 0), v)
                raw.add(k)
            if b.excl:
                for k, v in b.r.items():
                    if k != eng:
                        need[k] = max(need.get(k, 0), v)
        for b in writes:
            for k, v in b.w.items():
                need[k] = max(need.get(k, 0), v)
            for k, v in b.r.items():
                need[k] = max(need.get(k, 0), v)
        st = self.stream[eng]
        seen = self.seen[eng]
        for k, v in need.items():
            if k == eng:
                if k in raw and eng != "pe" and seen.get(k, 0) < v:
                    assert v <= self.tick[k], ("unsignaled same-engine dep", eng, k, v)
                    st.append(("w", k, v))
                    seen[k] = v
                continue
            if seen.get(k, 0) >= v:
                continue
            if k in self.tick:
                assert v <= self.tick[k], ("dep on unsignaled op", eng, k, v, self.tick[k])
            st.append(("w", k, v))
            seen[k] = v

    def op(self, eng, fn, reads=(), writes=(), signal=True):
        self._deps(eng, reads, writes)
        if signal:
            self.tick[eng] += 1
            ev = self.tick[eng]
            self.pending[eng] = False
        else:
            ev = self.tick[eng] + 1
            self.pending[eng] = True
        self.stream[eng].append(("o", fn, signal))
        for b in writes:
            b.w[eng] = max(b.w.get(eng, 0), ev)
        for b in reads:
            b.r[eng] = max(b.r.get(eng, 0), ev)

    def dma(self, q, out, in_, reads=(), writes=(), sem=None, accum=None):
        assert sem is not None
        self.sem_for(sem)
        self._deps(q, reads, writes)
        self.semval[sem] += 16
        v = self.semval[sem]
        self.stream[q].append(("d", out, in_, sem, accum))
        for b in writes:
            b.w[sem] = max(b.w.get(sem, 0), v)
        for b in reads:
            b.r[sem] = max(b.r.get(sem, 0), v)

    def wait_all(self, eng, bufs):
        self._deps(eng, bufs, ())

    def mm(self, out, lhsT, rhs, start, stop, reads, writes, signal=False):
        self.op("pe", lambda e: e.matmul(out, lhsT=lhsT, rhs=rhs, start=start, stop=stop),
                reads, writes, signal)

    def tr(self, out, in_, ident, reads, writes, signal=False):
        self.op("pe", lambda e: e.transpose(out, in_, ident), reads, writes, signal)

    def act(self, out, in_, func, reads, writes, bias=None, scale=None, accum_out=None):
        kw = {}
        if bias is not None:
            kw["bias"] = bias
        if scale is not None:
            kw["scale"] = scale
        if accum_out is not None:
            kw["accum_out"] = accum_out
        self.op("act", lambda e: e.activation(out=out, in_=in_, func=func, **kw), reads, writes)

    def tt(self, eng, out, in0, in1, op, reads, writes):
        self.op(eng, lambda e: e.tensor_tensor(out=out, in0=in0, in1=in1, op=op), reads, writes)

    def ts(self, eng, out, in0, s1, s2, op0, op1, reads, writes):
        if op1 is None:
            self.op(eng, lambda e: e.tensor_scalar(out=out, in0=in0, scalar1=s1, scalar2=None, op0=op0),
                    reads, writes)
        else:
            self.op(eng, lambda e: e.tensor_scalar(out=out, in0=in0, scalar1=s1, scalar2=s2, op0=op0, op1=op1),
                    reads, writes)

    def stt(self, eng, out, in0, scalar, in1, op0, op1, reads, writes):
        self.op(eng, lambda e: e.scalar_tensor_tensor(out=out, in0=in0, scalar=scalar, in1=in1, op0=op0, op1=op1),
                reads, writes)

    def copy(self, eng, out, in_, reads, writes):
        if eng == "act":
            self.op("act", lambda e: e.copy(out=out, in_=in_), reads, writes)
        else:
            self.op(eng, lambda e: e.tensor_copy(out=out, in_=in_), reads, writes)

    def memset(self, eng, ap, val, writes):
        self.op(eng, lambda e: e.memset(ap, val), (), writes)

    def recip(self, out, in_, reads, writes):
        self.op("dve", lambda e: e.reciprocal(out=out, in_=in_), reads, writes)

    def emit(self, final_bufs):
        nc = self.nc
        self._deps("sp", final_bufs, final_bufs)
        for e in self.CE:
            self.sem_for(e)
        sems = self.sems
        streams = self.stream

        def replay(eng_name):
            def run(eng):
                for it in streams[eng_name]:
                    if it[0] == "w":
                        eng.wait_ge(sems[it[1]], it[2])
                    elif it[0] == "o":
                        ins = it[1](eng)
                        if it[2]:
                            ins.then_inc(sems[eng_name], 1)
                    else:
                        if it[4] is None:
                            eng.dma_start(out=it[1], in_=it[2]).then_inc(sems[it[3]], 16)
                        else:
                            eng.dma_start(out=it[1], in_=it[2], accum_op=it[4]).then_inc(sems[it[3]], 16)
            return run

        with nc.Block() as block:
            block.tensor(replay("pe"))
            block.scalar(replay("act"))
            block.vector(replay("dve"))
            block.gpsimd(replay("pool"))
            block.sync(replay("sp"))


def bc(ap, shape):
    return ap.to_broadcast(list(shape))


def build_program(debug=None, stop_after=None):
    nc = bass.Bass("TRN2", target_bir_lowering=False)
    P = Prog(nc)
    KB = 1024

    def din(name, shape):
        return nc.dram_tensor(name, list(shape), F32, kind="ExternalInput").ap()

    x_d = din("x", [S, D])
    win_d = din("w_in_p", [D, WIN_COLS])
    wuq_d = din("w_uq_p", [768, 8 * 384])
    wkn_d = din("w_kn_p", [512, 1024])
    wv_d = din("w_v_p", [512, 1024])
    wout_d = din("w_out_b", [8, 128, 4096])
    wgate_d = din("w_gate_b", [NJ, 128, 2048])
    wup_d = din("w_up_b", [NJ, 128, 2048])
    wdown_d = din("w_down_b", [4 * 11, 128, 2048])

    def dscr(name, shape):
        return nc.dram_tensor(name, list(shape), BF16, kind="Internal").ap()

    wout_s = dscr("w_out_s", [8, 128, 4096])
    wgate_s = dscr("w_gate_s", [NJ, 128, 2048])
    wup_s = dscr("w_up_s", [NJ, 128, 2048])
    wdown_s = dscr("w_down_s", [4 * 11, 128, 2048])
    cvec_d = din("cvec", [128, 278])
    cbc_d = din("cbc", [128, 48 + 1024])
    gpost_d = din("gpost", [2, 128, 2048])
    tab_d = din("ropetab", [2, 128, S])
    mask_d = din("masks", [128, 4, 128])
    out_d = nc.dram_tensor("out", [S, D], F32, kind="ExternalOutput").ap()
    dbg_out = {}
    if debug:
        for name, shape in debug.items():
            dbg_out[name] = nc.dram_tensor("dbg_" + name, list(shape), F32, kind="ExternalOutput").ap()

    O_CONST = SB_BASE
    O_HT = O_CONST + 13312
    O_XS = O_HT + 64 * KB
    O_AO = O_XS + 32 * KB
    O_R3 = O_AO + 32 * KB
    O_R4 = O_R3 + 48 * KB

    o = O_CONST
    cvec, b_cvec = P.alloc("cvec", [128, 278], F32, o, dma=True); o += 1120
    cbc, b_cbc = P.alloc("cbc", [128, 48 + 1024], F32, o, dma=True); o += (48 + 1024) * 4
    masks, b_masks = P.alloc("masks", [128, 4, 128], F32, o, dma=True); o += 2048
    mbf, b_mbf = P.alloc("mbf", [128, 4, 128], BF16, o, dma=True); o += 1024
    aah, b_aah = P.alloc("aah", [128, 16, 16], BF16, o); o += 512
    aal, b_aal = P.alloc("aal", [128, 16, 16], BF16, o); o += 512
    aalf, b_aalf = P.alloc("aalf", [128, 16, 16], F32, o); o += 1024
    stat, b_stat = P.alloc("stat", [128, 64], F32, o); o += 256
    dtt, b_dtt = P.alloc("dtt", [128, 16, 16], F32, o); o += 1024
    aa, b_aa = P.alloc("aa", [128, 16, 16], F32, o); o += 1024
    abc, b_abc = P.alloc("abc", [128, 16], F32, o); o += 64
    halo, b_halo = P.alloc("halo", [128, NJ, 2], F32, o); o += 352
    assert o <= O_HT, o
    ident_f = masks[:, 0, :]
    ones_f = masks[:, 1, :]
    Um = masks[:, 2, :]
    Tm = masks[:, 3, :]
    ident_b = mbf[:, 0, :]
    ones_b = mbf[:, 1, :]
    Um_b = mbf[:, 2, :]
    Tm_b = mbf[:, 3, :]
    C_GPRE, C_GQ, C_GKV, C_CW, C_CB, C_GF, C_FW, C_FB = 0, 16, 22, 26, 74, 86, 102, 234
    B_DTB, B_ALOG, B_DSK, B_NG = 0, 16, 32, 48

    P.dma("sp", cvec[:], cvec_d, writes=[b_cvec], sem=b_cvec.sem)
    P.dma("sp", cbc[:], cbc_d, writes=[b_cbc], sem=b_cbc.sem)
    P.dma("sp", masks[:], mask_d, writes=[b_masks], sem=b_masks.sem)
    P.dma("pool", mbf[:], mask_d, writes=[b_mbf], sem=b_mbf.sem)

    pb = []
    pbb = []
    for i in range(8):
        t = nc.alloc_psum_tensor("pb%d" % i, [128, 512], F32)
        pb.append(t)
        pbb.append(P.newbuf("pb%d" % i))
        pbb[-1].excl = True

    b_scr = P.newbuf("scratch")
    conv_jobs = [(wout_s[i], wout_d[i]) for i in range(8)]
    for j in range(NJ):
        conv_jobs.append((wgate_s[j], wgate_d[j]))
        conv_jobs.append((wup_s[j], wup_d[j]))
    conv_jobs += [(wdown_s[i], wdown_d[i]) for i in range(44)]
    conv_pos = [0]

    def conv_some(n):
        for _ in range(n):
            if conv_pos[0] < len(conv_jobs):
                o_, i_ = conv_jobs[conv_pos[0]]
                conv_pos[0] += 1
                P.dma("pool", o_, i_, writes=[b_scr], sem="conv")

    def rstd_from_ssq(eng, out, in_, n, reads, writes):
        P.ts("dve", out, in_, 1.0 / n, EPS, ALU.mult, ALU.add, reads, writes)
        P.act(out, out, AF.Ln, writes, writes)
        P.act(out, out, AF.Exp, writes, writes, scale=-0.5)

    hT, b_hT_all = P.alloc("hT", [128, KC, S], BF16, O_HT)
    b_hT = [P.sub(b_hT_all, "hT%d" % i) for i in range(NT)]
    xsl = []
    for i in range(2):
        t, b = P.alloc("xsl%d" % i, [128, D], F32, O_R3 + i * 8 * KB, dma=True)
        xsl.append((t, b))
    xn = []
    for i in range(2):
        t, b = P.alloc("xn%d" % i, [128, D], BF16, O_R3 + 16 * KB + i * 4 * KB)
        xn.append((t, b))
    junk, b_junk = P.alloc("junk", [128, D], BF16, O_R3 + 24 * KB)

    def emit_a0(tt):
        xt, xb = xsl[tt % 2]
        nt_, nb_ = xn[tt % 2]
        P.dma("sp", xt[:], x_d[tt * 128:(tt + 1) * 128, :], writes=[xb], sem=xb.sem)
        P.act(junk[:], xt[:], AF.Square, [xb], [b_junk, b_stat], accum_out=stat[:, 0:1])
        rstd_from_ssq("dve", stat[:, 1:2], stat[:, 0:1], D, [b_stat], [b_stat])
        P.act(nt_[:], xt[:], AF.Copy, [xb, b_stat], [nb_], scale=stat[:, 1:2])
        for q in range(4):
            bank = (tt * 4 + q) % 4
            pt = pb[bank][:].bitcast(BF16)
            for i in range(4):
                kc = q * 4 + i
                P.tr(pt[:, i * 128:(i + 1) * 128], nt_[:, kc * 128:(kc + 1) * 128], ident_b,
                     [nb_, b_mbf], [pbb[bank]], signal=(i == 3))
            P.tt("dve", hT[:, q * 4:(q + 1) * 4, tt * 128:(tt + 1) * 128],
                 pt[:, 0:512].rearrange("p (a b) -> p a b", a=4),
                 bc(cvec[:, C_GPRE + q * 4:C_GPRE + q * 4 + 4].unsqueeze(2), [128, 4, 128]),
                 ALU.mult, [pbb[bank], b_cvec], [b_hT[tt]])

    def dump(name, ap, bufs):
        if debug and name in dbg_out:
            b = P.newbuf("dbg_" + name, dma=True)
            P.dma("pool", dbg_out[name], ap, reads=bufs, writes=[b], sem=b.sem)
            final_bufs.append(b)

    final_bufs = []

    wsl = []
    for i in range(2):
        t, b = P.alloc("wsl%d" % i, [128, KC, 256], BF16, O_R4 + i * 8 * KB, dma=True)
        wsl.append((t, b))
    wctr = [0]

    def load_w(src_ap, kc, ncols):
        t, b = wsl[wctr[0] % 2]
        wctr[0] += 1
        P.dma("pool", t[:, 0:kc, 0:ncols], src_ap.rearrange("(kc p) c -> p kc c", p=128), writes=[b], sem=b.sem)
        conv_some(2)
        return t, b

    xsT, b_xsT_all = P.alloc("xsT", [128, 8, S], BF16, O_XS)
    b_xsT = [P.sub(b_xsT_all, "xsT%d" % i) for i in range(NT)]
    BCT, b_BCT_all = P.alloc("BCT", [128, 4, S], BF16, O_R3)
    b_BCT = [P.sub(b_BCT_all, "BCT%d" % i) for i in range(NT)]
    zs, b_zs_all = P.alloc("zs", [128, NT, 1024], BF16, O_R3 + 16 * KB)
    b_zs = [P.sub(b_zs_all, "zs%d" % i) for i in range(NT)]
    pcb = []
    for i in range(2):
        t, b = P.alloc("pcb%d" % i, [128, 3 + 512], F32, O_AO + i * 2080)
        pcb.append((t, b))
    cacc_l = [P.alloc("cacc0", [128, 512], F32, O_AO + 2 * 2080),
              P.alloc("cacc1", [128, 512], F32, O_AO + 2 * 2080 + 2048)]

    bankc = [0]

    def next_bank(lo=0, n=8):
        b = lo + bankc[0] % n
        bankc[0] += 1
        return b

    XBC0 = 1536
    hal, b_hal = P.alloc("hal", [128, 12, 3], F32, O_AO + 2 * 2080 + 2 * 2048)
    pcs = [0]
    pend_silu = [None]

    def xbc_unit(wt, wb, jb, blk, tg):
        bank = next_bank()
        for kc in range(KC):
            P.mm(pb[bank][:], wt[:, kc, jb * 128:(jb + 1) * 128], hT[:, kc, tg * 512:(tg + 1) * 512],
                 kc == 0, kc == KC - 1, [wb] + b_hT[tg * 4:tg * 4 + 4], [pbb[bank]], signal=(kc == KC - 1))
        pt_, pb_ = pcb[pcs[0] % 2]
        cacc, b_cacc = cacc_l[pcs[0] % 2]
        pcs[0] += 1
        if tg == 0:
            P.memset("dve", pt_[:, 0:3], 0.0, [pb_])
        else:
            P.copy("dve", pt_[:, 0:3], hal[:, blk, :], [b_hal], [pb_])
        P.copy("act", pt_[:, 3:515], pb[bank][:], [pbb[bank]], [pb_])
        if pend_silu[0] is not None:
            pend_silu[0]()
            pend_silu[0] = None
        if tg < 3:
            P.copy("dve", hal[:, blk, :], pt_[:, 512:515], [pb_], [b_hal])
        cw = C_CW + blk * 4
        P.ts("dve", cacc[:], pt_[:, 3:515], cvec[:, cw + 3:cw + 4], cvec[:, C_CB + blk:C_CB + blk + 1],
             ALU.mult, ALU.add, [pb_, b_cvec], [b_cacc])
        for j in range(3):
            P.stt("dve", cacc[:], pt_[:, j:j + 512], cvec[:, cw + j:cw + j + 1], cacc[:],
                  ALU.mult, ALU.add, [pb_, b_cacc, b_cvec], [b_cacc])
        if blk < 8:
            dst = xsT[:, blk, tg * 512:(tg + 1) * 512]
            wbufs = b_xsT[tg * 4:tg * 4 + 4]
        else:
            dst = BCT[:, blk - 8, tg * 512:(tg + 1) * 512]
            wbufs = b_BCT[tg * 4:tg * 4 + 4]
        pend_silu[0] = lambda: P.act(dst, cacc[:], AF.Silu, [b_cacc], wbufs)

    w01 = [load_w(win_d[:, XBC0 + grp * 256: XBC0 + (grp + 1) * 256], KC, 256) for grp in range(2)]
    for tt in range(4):
        emit_a0(tt)
    for tg in range(4):
        if tg + 1 < 4:
            for tt in range(4 * (tg + 1), 4 * (tg + 1) + 4):
                emit_a0(tt)
        for grp in range(2):
            for jb in range(2):
                xbc_unit(w01[grp][0], w01[grp][1], jb, grp * 2 + jb, tg)
    for grp in range(2, 6):
        wt, wb = load_w(win_d[:, XBC0 + grp * 256: XBC0 + (grp + 1) * 256], KC, 256)
        for jb in range(2):
            for tg in range(4):
                xbc_unit(wt, wb, jb, grp * 2 + jb, tg)
    pend_silu[0]()
    pend_silu[0] = None

    Z0 = 3072
    for grp in range(4):
        wt, wb = load_w(win_d[:, Z0 + grp * 256: Z0 + (grp + 1) * 256], KC, 256)
        for tt in range(NT):
            bank = next_bank()
            for kc in range(KC):
                P.mm(pb[bank][:, 0:256], hT[:, kc, tt * 128:(tt + 1) * 128], wt[:, kc, :],
                     kc == 0, kc == KC - 1, [wb, b_hT[tt]], [pbb[bank]], signal=(kc == KC - 1))
            P.act(zs[:, tt, grp * 256:(grp + 1) * 256], pb[bank][:, 0:256], AF.Silu, [pbb[bank]], [b_zs[tt]])
    wt, wb = load_w(win_d[:, 4096:4112], KC, 16)
    for tt in range(NT):
        bank = next_bank()
        for kc in range(KC):
            P.mm(pb[bank][:, 0:16], hT[:, kc, tt * 128:(tt + 1) * 128], wt[:, kc, 0:16],
                 kc == 0, kc == KC - 1, [wb, b_hT[tt]], [pbb[bank]], signal=(kc == KC - 1))
        P.tt("dve", dtt[:, tt, :], pb[bank][:, 0:16], cbc[:, B_DTB:B_DTB + 16], ALU.add, [pbb[bank], b_cbc], [b_dtt])
    P.act(dtt[:], dtt[:], AF.Exp, [b_dtt], [b_dtt])
    P.act(dtt[:], dtt[:], AF.Ln, [b_dtt], [b_dtt], bias=1.0)
    P.act(abc[:], cbc[:, B_ALOG:B_ALOG + 16], AF.Exp, [b_cbc], [b_abc])
    P.stt("dve", aa[:], dtt[:], -1.0, bc(abc[:].unsqueeze(1), [128, 16, 16]), ALU.mult, ALU.mult,
          [b_dtt, b_abc], [b_aa])
    P.copy("dve", aah[:], aa[:], [b_aa], [b_aah])
    P.tt("dve", aalf[:], aa[:], aah[:], ALU.subtract, [b_aa, b_aah], [b_aalf])
    P.copy("dve", aal[:], aalf[:], [b_aalf], [b_aal])

    dump("xsT", xsT[:, 0, :], b_xsT)
    dump("BCT", BCT[:, 0, :], b_BCT)
    dump("zs", zs[:, 0, :], b_zs)
    dump("dtt", dtt[:].rearrange("p a b -> p (a b)"), [b_dtt])
    dump("hT", hT[:, 0, :], b_hT)

    if stop_after == "A1":
        P.emit(final_bufs)
        return nc

    o = O_AO
    lallh, b_lall = P.alloc("lallh", [128, 16, 128], BF16, o); o += 4 * KB
    lalll, b_lalll = P.alloc("lalll", [128, 16, 128], BF16, o); o += 4 * KB
    Ef, b_Ef = P.alloc("Ef", [128, 16, 128], BF16, o); o += 4 * KB
    Wb_l = [P.alloc("Wb0", [128, 16, 128], BF16, o)]; o += 4 * KB
    Wb_l.append(P.alloc("Wb1", [128, 16, 128], BF16, o)); o += 4 * KB
    xstok_l = [P.alloc("xstok0", [128, 1024], BF16, o)]; o += 2 * KB
    xdt_l = [P.alloc("xdt0", [128, 1024], BF16, o)]; o += 2 * KB
    xdtd_l = [P.alloc("xdtd0", [128, 1024], BF16, o)]; o += 2 * KB
    Sst, b_Sst = P.alloc("Sst", [128, 1024], F32, o); o += 4 * KB
    Sbf, b_Sbf = P.alloc("Sbf", [128, 1024], BF16, o); o += 2 * KB
    assert o <= O_AO + 32 * KB
    o = O_R4 + 0
    ycomb, b_ycomb = P.alloc("ycomb", [128, 1024], F32, o); o += 4 * KB
    ytmp, b_ytmp = P.alloc("ytmp", [128, 1024], F32, o); o += 4 * KB
    btok, b_btok = P.alloc("btok", [128, 1024], BF16, o); o += 2 * KB
    Btok_l = [P.alloc("Btok0", [128, 256], BF16, o)]; o += 512
    Btok_l.append(P.alloc("Btok1", [128, 256], BF16, o)); o += 512
    Gm, b_Gm = P.alloc("Gm", [128, 2, 128], F32, o); o += 1024
    eacs_l = [P.alloc("eacs0", [128, 32], F32, o)]; o += 128
    eacs_l.append(P.alloc("eacs1", [128, 32], F32, o)); o += 128
    sq2, b_sq2 = P.alloc("sq2", [128, 4], F32, o); o += 32
    xstok_l.append(P.alloc("xstok1", [128, 1024], BF16, o)); o += 2 * KB
    xdt_l.append(P.alloc("xdt1", [128, 1024], BF16, o)); o += 2 * KB
    xdtd_l.append(P.alloc("xdtd1", [128, 1024], BF16, o)); o += 2 * KB
    assert o <= SB_END, o
    boutT = xsT
    b_boutT = b_xsT
    ngb = cbc[:, B_NG:B_NG + 1024]
    import os
    NCH = int(os.environ.get('NCH', NT))

    def ssd_prep(c):
        cs = slice(c * 128, (c + 1) * 128)
        conv_some(3)
        xstok, b_xstok = xstok_l[c % 2]
        xdt, b_xdt = xdt_l[c % 2]
        xdtd, b_xdtd = xdtd_l[c % 2]
        Wb, b_Wb = Wb_l[c % 2]
        Btok, b_Btok = Btok_l[c % 2]
        eacs, b_eacs = eacs_l[c % 2]
        bk = next_bank()
        pt = pb[bk][:].bitcast(BF16)
        for j in range(8):
            P.tr(pt[:, j * 128:(j + 1) * 128], xsT[:, j, cs], ident_b, [b_xsT[c], b_mbf], [pbb[bk]], signal=(j == 7))
        P.copy("act", xstok[:], pt[:, 0:1024], [pbb[bk]], [b_xstok])
        P.tt("dve", xdt[:].rearrange("p (h q) -> p h q", h=16), pt[:, 0:1024].rearrange("p (h q) -> p h q", h=16),
             bc(dtt[:, c, :].unsqueeze(2), [128, 16, 64]), ALU.mult, [pbb[bk], b_dtt], [b_xdt])
        bk2 = next_bank()
        pt2 = pb[bk2][:].bitcast(BF16)
        for g in range(2):
            P.tr(pt2[:, g * 128:(g + 1) * 128], BCT[:, g, cs], ident_b, [b_BCT[c], b_mbf], [pbb[bk2]], signal=(g == 1))
        P.copy("act", Btok[:], pt2[:, 0:256], [pbb[bk2]], [b_Btok])
        bk3 = next_bank()
        for g in range(2):
            P.mm(pb[bk3][:, g * 128:(g + 1) * 128], BCT[:, g, cs], BCT[:, 2 + g, cs], True, True,
                 [b_BCT[c]], [pbb[bk3]], signal=(g == 1))
        P.tt("dve", Gm[:], pb[bk3][:, 0:256].rearrange("p (g l) -> p g l", g=2),
             bc(Tm.unsqueeze(1), [128, 2, 128]), ALU.mult, [pbb[bk3], b_masks], [b_Gm])
    def ssd_prep_a2(c):
        P.tt("pool", lallh[:], bc(Um_b.unsqueeze(1), [128, 16, 128]), bc(aah[:, c, :].unsqueeze(2), [128, 16, 128]),
             ALU.mult, [b_mbf, b_aah], [b_lall])
        P.tt("pool", lalll[:], bc(Um_b.unsqueeze(1), [128, 16, 128]), bc(aal[:, c, :].unsqueeze(2), [128, 16, 128]),
             ALU.mult, [b_mbf, b_aal], [b_lalll])
        for q in range(4):
            bk4 = next_bank()
            for i in range(4):
                h = q * 4 + i
                P.mm(pb[bk4][:, i * 128:(i + 1) * 128], lallh[:, h, :], Tm_b, True, False,
                     [b_lall, b_mbf], [pbb[bk4]], signal=False)
                P.mm(pb[bk4][:, i * 128:(i + 1) * 128], lalll[:, h, :], Tm_b, False, True,
                     [b_lalll, b_mbf], [pbb[bk4]], signal=(i == 3))
            P.act(Ef[:, q * 4:(q + 1) * 4, :], pb[bk4][:].rearrange("p (a b) -> p a b", a=4), AF.Exp,
                  [pbb[bk4]], [b_Ef])
    def ssd_prep_b(c):
        xdt, b_xdt = xdt_l[c % 2]
        xdtd, b_xdtd = xdtd_l[c % 2]
        Wb, b_Wb = Wb_l[c % 2]
        eacs, b_eacs = eacs_l[c % 2]
        for g in range(2):
            P.tt("dve", Wb[:, g * 8:(g + 1) * 8, :], Ef[:, g * 8:(g + 1) * 8, :],
                 bc(Gm[:, g, :].unsqueeze(1), [128, 8, 128]), ALU.mult, [b_Ef, b_Gm], [b_Wb])
        bk5 = next_bank()
        P.mm(pb[bk5][:, 0:16], Tm_b, aah[:, c, :], True, False, [b_mbf, b_aah], [pbb[bk5]], signal=False)
        P.mm(pb[bk5][:, 0:16], Tm_b, aal[:, c, :], False, True, [b_mbf, b_aal], [pbb[bk5]], signal=False)
        P.mm(pb[bk5][:, 16:32], ones_b, aah[:, c, :], True, False, [b_mbf, b_aah], [pbb[bk5]], signal=False)
        P.mm(pb[bk5][:, 16:32], ones_b, aal[:, c, :], False, True, [b_mbf, b_aal], [pbb[bk5]], signal=True)
        P.act(eacs[:], pb[bk5][:, 0:32], AF.Exp, [pbb[bk5]], [b_eacs])
        P.tt("dve", xdtd[:].rearrange("p (h q) -> p h q", h=16), xdt[:].rearrange("p (h q) -> p h q", h=16),
             bc(Ef[:, :, 127:128], [128, 16, 64]), ALU.mult, [b_xdt, b_Ef], [b_xdtd])

    def ssd_finish(c):
        cs = slice(c * 128, (c + 1) * 128)
        xstok, b_xstok = xstok_l[c % 2]
        xdt, b_xdt = xdt_l[c % 2]
        xdtd, b_xdtd = xdtd_l[c % 2]
        Wb, b_Wb = Wb_l[c % 2]
        Btok, b_Btok = Btok_l[c % 2]
        eacs, b_eacs = eacs_l[c % 2]
        bko = [next_bank(), next_bank()]
        bkd = [next_bank(), next_bank()]
        if c > 0:
            for g in range(2):
                P.mm(pb[bko[g]][:], BCT[:, 2 + g, cs], Sbf[:, g * 512:(g + 1) * 512], True, True,
                     [b_BCT[c], b_Sbf], [pbb[bko[g]]], signal=True)
        for g in range(2):
            for i in range(8):
                h = g * 8 + i
                P.mm(pb[bkd[g]][:, i * 64:(i + 1) * 64], Wb[:, h, :], xdt[:, h * 64:(h + 1) * 64], True, True,
                     [b_Wb, b_xdt], [pbb[bkd[g]]], signal=(i == 7))
        fin_state[c] = (bko, bkd)

    def ssd_finish_ew(c):
        xstok, b_xstok = xstok_l[c % 2]
        eacs, b_eacs = eacs_l[c % 2]
        bko, bkd = fin_state[c]
        for g in range(2):
            hs = slice(g * 512, (g + 1) * 512)
            if c > 0:
                P.tt("dve", ycomb[:, hs].rearrange("p (h q) -> p h q", h=8),
                     pb[bko[g]][:].rearrange("p (h q) -> p h q", h=8),
                     bc(eacs[:, g * 8:(g + 1) * 8].unsqueeze(2), [128, 8, 64]), ALU.mult,
                     [pbb[bko[g]], b_eacs], [b_ycomb])
                P.tt("dve", ycomb[:, hs], ycomb[:, hs], pb[bkd[g]][:], ALU.add, [b_ycomb, pbb[bkd[g]]], [b_ycomb])
            else:
                P.copy("dve", ycomb[:, hs], pb[bkd[g]][:], [pbb[bkd[g]]], [b_ycomb])
        P.tt("dve", ytmp[:].rearrange("p (h q) -> p h q", h=16), xstok[:].rearrange("p (h q) -> p h q", h=16),
             bc(cbc[:, B_DSK:B_DSK + 16].unsqueeze(2), [128, 16, 64]), ALU.mult, [b_xstok, b_cbc], [b_ytmp])
        P.tt("dve", ycomb[:], ycomb[:], ytmp[:], ALU.add, [b_ycomb, b_ytmp], [b_ycomb])
        P.tt("dve", ycomb[:], ycomb[:], zs[:, c, :], ALU.mult, [b_ycomb, b_zs[c]], [b_ycomb])
        for g in range(2):
            P.act(btok[:, g * 512:(g + 1) * 512], ycomb[:, g * 512:(g + 1) * 512], AF.Square, [b_ycomb],
                  [b_btok, b_sq2], accum_out=sq2[:, g:g + 1])
    def ssd_finish_b(c):
        cs = slice(c * 128, (c + 1) * 128)
        xdtd, b_xdtd = xdtd_l[c % 2]
        Btok, b_Btok = Btok_l[c % 2]
        eacs, b_eacs = eacs_l[c % 2]
        rstd_from_ssq("dve", sq2[:, 2:4], sq2[:, 0:2], 512, [b_sq2], [b_sq2])
        for g in range(2):
            P.stt("dve", btok[:, g * 512:(g + 1) * 512], ycomb[:, g * 512:(g + 1) * 512], sq2[:, 2 + g:3 + g],
                  ngb[:, g * 512:(g + 1) * 512], ALU.mult, ALU.mult, [b_ycomb, b_sq2, b_cbc], [b_btok])
        bk6 = next_bank()
        pt6 = pb[bk6][:].bitcast(BF16)
        for j in range(8):
            P.tr(pt6[:, j * 128:(j + 1) * 128], btok[:, j * 128:(j + 1) * 128], ident_b, [b_btok, b_mbf], [pbb[bk6]],
                 signal=(j == 7))
        P.copy("act", boutT[:, :, cs], pt6[:, 0:1024].rearrange("p (j t) -> p j t", j=8), [pbb[bk6]], [b_boutT[c]])
        if c < NT - 1:
            bks = [next_bank(), next_bank()]
            for g in range(2):
                P.mm(pb[bks[g]][:], Btok[:, g * 128:(g + 1) * 128], xdtd[:, g * 512:(g + 1) * 512], True, True,
                     [b_Btok, b_xdtd], [pbb[bks[g]]], signal=True)
            P.tt("dve", Sst[:].rearrange("p (h q) -> p h q", h=16), Sst[:].rearrange("p (h q) -> p h q", h=16),
                 bc(eacs[:, 16:32].unsqueeze(2), [128, 16, 64]), ALU.mult, [b_Sst, b_eacs], [b_Sst])
            for g in range(2):
                P.tt("dve", Sst[:, g * 512:(g + 1) * 512], Sst[:, g * 512:(g + 1) * 512], pb[bks[g]][:], ALU.add,
                     [b_Sst, pbb[bks[g]]], [b_Sst])
            P.copy("act", Sbf[:], Sst[:], [b_Sst], [b_Sbf])

    P.memset("dve", Sst[:], 0.0, [b_Sst])
    fin_state = {}
    ssd_prep(0)
    ssd_prep_a2(0)
    ssd_prep_b(0)
    for c in range(NCH):
        if c + 1 < NCH:
            ssd_prep(c + 1)
        ssd_finish(c)
        if c + 1 < NCH:
            ssd_prep_a2(c + 1)
        ssd_finish_ew(c)
        if c + 1 < NCH:
            ssd_prep_b(c + 1)
        ssd_finish_b(c)

    dump("boutT", boutT[:, 0, :], b_boutT)
    dump("boutT7", boutT[:, 7, :], b_boutT)
    if stop_after == "A2":
        P.emit(final_bufs)
        return nc

    cqn, b_cqn = P.alloc("cqn", [128, 6, S], BF16, O_R3)
    ckvn, b_ckvn = P.alloc("ckvn", [128, 4, S], BF16, O_R3 + 24 * KB)
    krT, b_krT = P.alloc("krT", [128, S], BF16, O_R3 + 40 * KB)
    wsl2 = []
    for i in range(2):
        t, b = P.alloc("wslb%d" % i, [128, KC, 256], BF16, O_R4 + i * 8 * KB, dma=True)
        wsl2.append((t, b))
    sqt = []
    for i in range(3):
        t, b = P.alloc("sqt%d" % i, [128, 512], BF16, O_AO + 20 * KB + i * KB)
        sqt.append((t, b))
    rsb, b_rsb = P.alloc("rsb", [128, 512], F32, O_AO + 18 * KB)
    tabA, b_tabA = P.alloc("tabA", [128, 2, S], F32, O_AO, dma=True)
    P.dma("sp", tabA[:, 0, :], tab_d[0], writes=[b_tabA], sem=b_tabA.sem)
    P.dma("sp", tabA[:, 1, :], tab_d[1], writes=[b_tabA], sem=b_tabA.sem)

    w2c = [0]

    def load_w2(src_ap, kc, ncols):
        t, b = wsl2[w2c[0] % 2]
        w2c[0] += 1
        P.dma("pool", t[:, 0:kc, 0:ncols], src_ap.rearrange("(kc p) c -> p kc c", p=128), writes=[b], sem=b.sem)
        conv_some(2)
        return t, b

    sqc = 0
    pend_ones = [None]
    for (dst, dbuf, col0, nblk, gcol, nfeat) in ((cqn, b_cqn, 0, 6, C_GQ, 768), (ckvn, b_ckvn, 768, 4, C_GKV, 512)):
        for grp in range(nblk // 2):
            wt, wb = load_w2(win_d[:, col0 + grp * 256: col0 + (grp + 1) * 256], KC, 256)
            for jb in range(2):
                blk = grp * 2 + jb
                for tg in range(4):
                    bank = next_bank(0, 4)
                    for kc in range(KC):
                        P.mm(pb[bank][:], wt[:, kc, jb * 128:(jb + 1) * 128], hT[:, kc, tg * 512:(tg + 1) * 512],
                             kc == 0, kc == KC - 1, [wb] + b_hT[tg * 4:tg * 4 + 4], [pbb[bank]], signal=(kc == KC - 1))
                    st_, sb_ = sqt[sqc % 3]
                    sqc += 1
                    P.act(st_[:], pb[bank][:], AF.Square, [pbb[bank]], [sb_])
                    P.ts("dve", dst[:, blk, tg * 512:(tg + 1) * 512], pb[bank][:], cvec[:, gcol + blk:gcol + blk + 1],
                         None, ALU.mult, None, [pbb[bank], b_cvec], [dbuf])
                    if pend_ones[0] is not None:
                        pend_ones[0]()
                    pend_ones[0] = (lambda tg=tg, st_=st_, sb_=sb_, blk=blk, nblk=nblk:
                                    P.mm(pb[4 + tg][:], ones_b, st_[:], blk == 0, blk == nblk - 1, [sb_, b_mbf],
                                         [pbb[4 + tg]], signal=True))
        pend_ones[0]()
        pend_ones[0] = None
        for tg in range(4):
            rstd_from_ssq("dve", rsb[:], pb[4 + tg][:], nfeat, [pbb[4 + tg]], [b_rsb])
            P.tt("dve", dst[:, :, tg * 512:(tg + 1) * 512], dst[:, :, tg * 512:(tg + 1) * 512],
                 bc(rsb[:].unsqueeze(1), [128, nblk, 512]), ALU.mult, [dbuf, b_rsb], [dbuf])
    wt, wb = load_w2(win_d[:, 1280:1536], KC, 256)
    rt1, b_rt1 = P.alloc("rt1", [128, 512], F32, O_AO + 16 * KB)
    for tg in range(4):
        bA = next_bank(0, 4)
        bB = next_bank(0, 4)
        for (bank, jb) in ((bA, 0), (bB, 1)):
            for kc in range(KC):
                P.mm(pb[bank][:], wt[:, kc, jb * 128:(jb + 1) * 128], hT[:, kc, tg * 512:(tg + 1) * 512],
                     kc == 0, kc == KC - 1, [wb] + b_hT[tg * 4:tg * 4 + 4], [pbb[bank]], signal=(kc == KC - 1))
        ts_ = slice(tg * 512, (tg + 1) * 512)
        P.tt("dve", rt1[:], pb[bA][:], tabA[:, 0, ts_], ALU.mult, [pbb[bA], b_tabA], [b_rt1])
        P.tt("dve", rsb[:], pb[bB][:], tabA[:, 1, ts_], ALU.mult, [pbb[bB], b_tabA], [b_rsb])
        P.tt("dve", krT[:, ts_], rt1[:], rsb[:], ALU.add, [b_rt1, b_rsb], [b_krT])

    dump("cqn", cqn[:, 0, :], [b_cqn])
    dump("ckvn", ckvn[:, 3, :], [b_ckvn])
    dump("krT", krT[:], [b_krT])
    if stop_after == "A3":
        P.emit(final_bufs)
        return nc

    o = O_HT
    tab, b_tab = P.alloc("tab", [128, 2, S], F32, o); o += 16 * KB
    P.copy("act", tab[:, 0, :], tabA[:, 0, :], [b_tabA], [b_tab])
    P.copy("act", tab[:, 1, :], tabA[:, 1, :], [b_tabA], [b_tab])
    hb = []
    for i in range(2):
        d = {}
        d["qn"], d["b_qn"] = P.alloc("qn%d" % i, [128, S], BF16, o); o += 4 * KB
        d["qr"], d["b_qr"] = P.alloc("qr%d" % i, [128, S], BF16, o); o += 4 * KB
        d["kn"], d["b_kn"] = P.alloc("kn%d" % i, [128, S], BF16, o); o += 4 * KB
        d["v"], d["b_v"] = P.alloc("v%d" % i, [128, NT, 128], BF16, o); o += 4 * KB
        hb.append(d)
    PT = []
    for i in range(4):
        t, b = P.alloc("PT%d" % i, [128, 512], BF16, o); o += KB
        PT.append((t, b))
    rec, b_rec = P.alloc("rec", [128, 512], F32, o); o += 2 * KB
    ra, b_ra = P.alloc("ra", [128, 512], F32, o); o += 2 * KB
    rb, b_rb = P.alloc("rb", [128, 512], F32, o); o += 2 * KB
    assert o <= O_HT + 64 * KB
    aoutT, b_aoutT = P.alloc("aoutT", [128, 8, S], BF16, O_AO)
    wq = []
    for i in range(2):
        t, b = P.alloc("wq%d" % i, [128, 6, 384], BF16, O_R4 + i * 8 * KB, dma=True)
        t2, b2 = P.alloc("wkv%d" % i, [128, 4, 256], BF16, O_R4 + i * 8 * KB + 4608, dma=True)
        wq.append((t, b, t2, b2))

    ptc = 0
    for h in range(8):
        d = hb[h % 2]
        wqt, wqb, wkt, wkb = wq[h % 2]
        P.dma("pool", wqt[:], wuq_d[:, h * 384:(h + 1) * 384].rearrange("(kc p) c -> p kc c", p=128),
              writes=[wqb], sem=wqb.sem)
        P.dma("pool", wkt[:, :, 0:128], wkn_d[:, h * 128:(h + 1) * 128].rearrange("(kc p) c -> p kc c", p=128),
              writes=[wkb], sem=wkb.sem)
        P.dma("pool", wkt[:, :, 128:256], wv_d[:, h * 128:(h + 1) * 128].rearrange("(kc p) c -> p kc c", p=128),
              writes=[wkb], sem=wkb.sem)
        conv_some(8)
        for tg in range(4):
            ts_ = slice(tg * 512, (tg + 1) * 512)
            bank = next_bank(0, 4)
            for kc in range(6):
                P.mm(pb[bank][:], wqt[:, kc, 0:128], cqn[:, kc, ts_], kc == 0, kc == 5, [wqb, b_cqn], [pbb[bank]],
                     signal=(kc == 5))
            P.copy("act", d["qn"][:, ts_], pb[bank][:], [pbb[bank]], [d["b_qn"]])
            bA = next_bank(0, 4)
            bB = next_bank(0, 4)
            for (bank, c0) in ((bA, 128), (bB, 256)):
                for kc in range(6):
                    P.mm(pb[bank][:], wqt[:, kc, c0:c0 + 128], cqn[:, kc, ts_], kc == 0, kc == 5, [wqb, b_cqn],
                         [pbb[bank]], signal=(kc == 5))
            P.tt("dve", ra[:], pb[bA][:], tab[:, 0, ts_], ALU.mult, [pbb[bA], b_tab], [b_ra])
            P.tt("dve", rb[:], pb[bB][:], tab[:, 1, ts_], ALU.mult, [pbb[bB], b_tab], [b_rb])
            P.tt("dve", d["qr"][:, ts_], ra[:], rb[:], ALU.add, [b_ra, b_rb], [d["b_qr"]])
            bank = next_bank(0, 4)
            for kc in range(4):
                P.mm(pb[bank][:], wkt[:, kc, 0:128], ckvn[:, kc, ts_], kc == 0, kc == 3, [wkb, b_ckvn], [pbb[bank]],
                     signal=(kc == 3))
            P.copy("act", d["kn"][:, ts_], pb[bank][:], [pbb[bank]], [d["b_kn"]])
            bank = next_bank(0, 4)
            for i in range(4):
                tt = tg * 4 + i
                for kc in range(4):
                    P.mm(pb[bank][:, i * 128:(i + 1) * 128], ckvn[:, kc, tt * 128:(tt + 1) * 128], wkt[:, kc, 128:256],
                         kc == 0, kc == 3, [wkb, b_ckvn], [pbb[bank]], signal=(kc == 3 and i == 3))
            P.copy("act", d["v"][:, tg * 4:(tg + 1) * 4, :], pb[bank][:].rearrange("p (a b) -> p a b", a=4),
                   [pbb[bank]], [d["b_v"]])
        for qg in range(4):
            bo = 4 + (qg % 2) * 2
            bs = bo + 1
            nkb = 4 * (qg + 1)
            def s_mm(kb):
                j = kb - 4 * qg
                q0 = 0 if j < 0 else 128 * j
                n = 512 - q0
                qs = slice(qg * 512 + q0, (qg + 1) * 512)
                ks = slice(kb * 128, (kb + 1) * 128)
                bank = next_bank(0, 4)
                P.mm(pb[bank][:, 0:n], d["kn"][:, ks], d["qn"][:, qs], True, False, [d["b_kn"], d["b_qn"]],
                     [pbb[bank]], signal=False)
                P.mm(pb[bank][:, 0:n], krT[:, ks], d["qr"][:, qs], False, True, [b_krT, d["b_qr"]],
                     [pbb[bank]], signal=True)
                return (bank, j, q0, n)

            nxt = s_mm(0)
            for kb in range(nkb):
                bank, j, q0, n = nxt
                if kb + 1 < nkb:
                    nxt = s_mm(kb + 1)
                pt_, ptb = PT[ptc % 4]
                ptc += 1
                P.act(pt_[:, 0:n], pb[bank][:, 0:n], AF.Exp, [pbb[bank]], [ptb], scale=SCALE)
                if j >= 0:
                    P.memset("dve", pt_[64:128, 0:64], 0.0, [ptb])
                P.mm(pb[bo][:, q0:512], d["v"][:, kb, :], pt_[:, 0:n], kb == 0, kb == nkb - 1, [d["b_v"], ptb],
                     [pbb[bo]], signal=False)
                P.mm(pb[bs][:, q0:512], ones_b, pt_[:, 0:n], kb == 0, kb == nkb - 1, [b_mbf, ptb],
                     [pbb[bs]], signal=True)
            P.recip(rec[:], pb[bs][:], [pbb[bs]], [b_rec])
            P.tt("dve", aoutT[:, h, qg * 512:(qg + 1) * 512], pb[bo][:], rec[:], ALU.mult, [pbb[bo], b_rec], [b_aoutT])

    dump("aoutT", aoutT[:, 0, :], [b_aoutT])
    dump("aoutT7", aoutT[:, 7, :], [b_aoutT])
    if stop_after == "A4":
        P.emit(final_bufs)
        return nc

    O_X1 = O_HT
    O_H2 = O_X1 + 32 * KB
    O_WG = O_H2 + 16 * KB
    assert O_WG + 16 * KB <= O_XS
    O_ACT = O_AO + 32 * KB
    O_WD = O_ACT + 44 * KB
    O_GT = O_WD + 8 * KB
    assert O_GT + 2080 + 2048 <= SB_END
    x1, b_x1_all = P.alloc("x1", [128, 4, D], F32, O_X1)
    h2T, b_h2T0 = P.alloc("h2T", [128, KC, 512], BF16, O_H2)
    actT, b_actT_all = P.alloc("actT", [128, NJ, 512], BF16, O_ACT)
    wos0 = []
    for i in range(2):
        t, b = P.alloc("wos%d" % i, [128, KC, 256], BF16, O_ACT + i * 8 * KB, dma=True)
        wos0.append((t, b))
    xn2b, b_xs2_0 = P.alloc("xn2b", [128, D], BF16, O_ACT + 16 * KB)
    xn2, b_xn2_0 = P.alloc("xn2", [128, D], BF16, O_ACT + 24 * KB)
    gbc, b_gbc_0 = P.alloc("gbc", [128, D], F32, O_ACT + 28 * KB, dma=True)
    junk3, b_junk3_0 = P.alloc("junk3", [128, D], BF16, O_ACT + 36 * KB)
    O_SP = O_GT + 2080 + 2048 + 192
    assert O_SP + 8 * KB <= SB_END
    NWG = 3
    wgs0 = []
    for i in range(NWG):
        off = O_WG + i * 8 * KB if i < 2 else O_SP
        t, b = P.alloc("wgs%d" % i, [128, KC, 128], BF16, off, dma=True)
        t2, b2 = P.alloc("wus%d" % i, [128, KC, 128], BF16, off + 4 * KB, dma=True)
        wgs0.append((t, b, t2, b2))
    NWD = 4
    wds = []
    for i in range(NWD):
        t, b = P.alloc("wds%d" % i, [128, 2, 512], BF16, O_WD + i * 2 * KB, dma=True)
        wds.append((t, b))
    conv_some(len(conv_jobs))
    ft0 = []
    for i in range(4):
        t, b = P.alloc("ft%d" % i, [128, D], F32, O_H2 + i * 8 * KB, dma=True)
        ft0.append((t, b))
    gt, b_gt = P.alloc("gt", [128, 2 + 512], F32, O_GT)
    gacc, b_gacc = P.alloc("gacc", [128, 512], F32, O_GT + 2080)
    ssqp, b_ssqp = P.alloc("ssqp", [128, 48], F32, O_GT + 2080 + 2048)
    st2, b_st2 = stat, b_stat
    P.memset("dve", halo[:], 0.0, [b_halo])
    mixT = lambda kc: (aoutT[:, kc, :] if kc < 8 else boutT[:, kc - 8, :])
    mix_bufs = [b_aoutT] + b_boutT
    woc = 0
    wgc = 0
    wdc = 0
    ngroups = 4 if stop_after is None else 1
    for g in range(ngroups):
        b_x1 = [P.newbuf("x1_%d_%d" % (g, i), b_x1_all.rng, sem="x1acc%d" % i) for i in range(4)]
        wos = [(t, P.rebuf(b)) for (t, b) in wos0]
        xn2_l = [(xn2, P.rebuf(b_xn2_0)), (xn2b, P.rebuf(b_xs2_0))]
        b_gbc = P.rebuf(b_gbc_0)
        b_junk3 = P.rebuf(b_junk3_0)
        b_h2T = P.rebuf(b_h2T0)
        P.dma("pool", gbc[:], gpost_d[0], writes=[b_gbc], sem=b_gbc.sem)
        b_outd = [None] * 4

        def outproj(tiles, cgs):
            nonlocal woc
            for cg in cgs:
                wt, wb = wos[woc % 2]
                woc += 1
                P.dma("sp", wt[:], wout_s[cg].rearrange("p (kc c) -> p kc c", kc=KC), reads=[b_scr],
                      writes=[wb], sem=wb.sem)
                for i in tiles:
                    tt = g * 4 + i
                    bank = next_bank(0, 4)
                    for kc in range(KC):
                        P.mm(pb[bank][:, 0:256], mixT(kc)[:, tt * 128:(tt + 1) * 128], wt[:, kc, :], kc == 0,
                             kc == KC - 1, [wb] + mix_bufs, [pbb[bank]], signal=(kc == KC - 1))
                    P.copy("act", x1[:, i, cg * 256:(cg + 1) * 256], pb[bank][:, 0:256], [pbb[bank]], [b_x1[i]])
                    P.act(junk3[:, 0:256], pb[bank][:, 0:256], AF.Square, [pbb[bank]], [b_junk3, b_ssqp],
                          accum_out=ssqp[:, i * 8 + cg:i * 8 + cg + 1])

        def step2_I(tiles):
            a, b = tiles[0], tiles[-1] + 1
            for i in tiles:
                P.op("dve", lambda e, i=i: e.reduce_sum(out=st2[:, 16 + i:17 + i], in_=ssqp[:, i * 8:i * 8 + 8],
                                                         axis=mybir.AxisListType.X), [b_ssqp], [b_st2])
            rstd_from_ssq("dve", st2[:, 20 + a:20 + b], st2[:, 16 + a:16 + b], D, [b_st2], [b_st2])
            for i in tiles:
                tt = g * 4 + i
                P.stt("dve", x1[:, i, :], x1[:, i, :], st2[:, 20 + i:21 + i], gbc[:], ALU.mult, ALU.mult,
                      [b_x1[i], b_st2, b_gbc], [b_x1[i]])
                P.dma("pool", x1[:, i, :], x_d[tt * 128:(tt + 1) * 128, :], writes=[b_x1[i]], sem=b_x1[i].sem,
                      accum=ALU.add)

        def step2_II(tiles):
            a, b = tiles[0], tiles[-1] + 1
            for i in tiles:
                P.act(junk3[:], x1[:, i, :], AF.Square, [b_x1[i]], [b_junk3, b_st2], accum_out=st2[:, 24 + i:25 + i])
            rstd_from_ssq("dve", st2[:, 28 + a:28 + b], st2[:, 24 + a:24 + b], D, [b_st2], [b_st2])
            for i in tiles:
                xn2_t, b_xn2_i = xn2_l[i % 2]
                P.act(xn2_t[:], x1[:, i, :], AF.Copy, [b_x1[i], b_st2], [b_xn2_i], scale=st2[:, 28 + i:29 + i])

        def step2_T(tiles):
            for i in tiles:
                tt = g * 4 + i
                xn2_t, b_xn2_i = xn2_l[i % 2]
                for q in range(4):
                    bank = next_bank(0, 4)
                    pt = pb[bank][:].bitcast(BF16)
                    for ii in range(4):
                        kc = q * 4 + ii
                        P.tr(pt[:, ii * 128:(ii + 1) * 128], xn2_t[:, kc * 128:(kc + 1) * 128], ident_b,
                             [b_xn2_i, b_mbf], [pbb[bank]], signal=(ii == 3))
                    P.tt("dve", h2T[:, q * 4:(q + 1) * 4, i * 128:(i + 1) * 128],
                         pt[:, 0:512].rearrange("p (a b) -> p a b", a=4),
                         bc(cvec[:, C_GF + q * 4:C_GF + q * 4 + 4].unsqueeze(2), [128, 4, 128]),
                         ALU.mult, [pbb[bank], b_cvec], [b_h2T])
                bo_ = P.newbuf("outd_%d_%d" % (g, i), sem="outd%d" % i)
                b_outd[i] = bo_
                P.dma("pool", out_d[tt * 128:(tt + 1) * 128, :], x1[:, i, :], reads=[b_x1[i]], writes=[bo_],
                      sem=bo_.sem)
                final_bufs.append(bo_)

        outproj((0, 1), range(8))
        step2_I((0, 1))
        outproj((2, 3), range(0, 4))
        step2_II((0, 1))
        outproj((2, 3), range(4, 8))
        step2_T((0, 1))
        step2_I((2, 3))
        step2_II((2, 3))
        step2_T((2, 3))
        if debug and g == 0:
            dump("x1", x1[:, 0, :], b_x1)
            dump("h2T", h2T[:, 0, :], [b_h2T])
        b_actT = [P.sub(b_actT_all, "actT_%d_%d" % (g, j)) for j in range(NJ)]
        wgs = [(t, P.rebuf(b), t2, P.rebuf(b2)) for (t, b, t2, b2) in wgs0]
        for j in range(NJ):
            wgt, wgb, wut, wub = wgs[wgc % NWG]
            wgc += 1
            P.dma("sp", wgt[:], wgate_s[j].rearrange("p (kc c) -> p kc c", kc=KC), reads=[b_scr],
                  writes=[wgb], sem=wgb.sem)
            P.dma("sp", wut[:], wup_s[j].rearrange("p (kc c) -> p kc c", kc=KC), reads=[b_scr],
                  writes=[wub], sem=wub.sem)
            bg = next_bank(0, 4)
            bu = next_bank(0, 4)
            for kc in range(KC):
                P.mm(pb[bg][:], wgt[:, kc, :], h2T[:, kc, :], kc == 0, kc == KC - 1, [wgb, b_h2T], [pbb[bg]],
                     signal=(kc == KC - 1))
            for kc in range(KC):
                P.mm(pb[bu][:], wut[:, kc, :], h2T[:, kc, :], kc == 0, kc == KC - 1, [wub, b_h2T], [pbb[bu]],
                     signal=(kc == KC - 1))
            P.copy("dve", gt[:, 0:2], halo[:, j, :], [b_halo], [b_gt])
            P.copy("act", gt[:, 2:514], pb[bg][:], [pbb[bg]], [b_gt])
            P.copy("dve", halo[:, j, :], gt[:, 512:514], [b_gt], [b_halo])
            fw = C_FW + j * 3
            P.ts("dve", gacc[:], gt[:, 2:514], cvec[:, fw + 2:fw + 3], cvec[:, C_FB + j:C_FB + j + 1],
                 ALU.mult, ALU.add, [b_gt, b_cvec], [b_gacc])
            for k in range(2):
                P.stt("dve", gacc[:], gt[:, k:k + 512], cvec[:, fw + k:fw + k + 1], gacc[:], ALU.mult, ALU.add,
                      [b_gt, b_gacc, b_cvec], [b_gacc])
            P.act(gacc[:], gacc[:], AF.Gelu_apprx_tanh, [b_gacc], [b_gacc])
            P.tt("dve", actT[:, j, :], gacc[:], pb[bu][:], ALU.mult, [b_gacc, pbb[bu]], [b_actT[j]])
        if debug and g == 0:
            dump("actT", actT[:, 0, :], b_actT)
        ft = [(t, P.rebuf(b)) for (t, b) in ft0]
        for cg in range(4):
            for kh in range(NJ // 2):
                wt, wb = wds[wdc % NWD]
                wdc += 1
                P.dma("sp", wt[:], wdown_s[cg * 11 + kh // 2][:, (kh % 2) * 1024:(kh % 2) * 1024 + 1024]
                      .rearrange("p (kc c) -> p kc c", kc=2), reads=[b_scr], writes=[wb], sem=wb.sem)
                for kk in range(2):
                    k = kh * 2 + kk
                    for i in range(4):
                        P.mm(pb[4 + i][:], actT[:, k, i * 128:(i + 1) * 128], wt[:, kk, :], k == 0, k == NJ - 1,
                             [wb, b_actT[k]], [pbb[4 + i]], signal=(k == NJ - 1) or (kk == 1 and i == 3))
            for i in range(4):
                fti, ftb = ft[i]
                P.copy("act", fti[:, cg * 512:(cg + 1) * 512], pb[4 + i][:], [pbb[4 + i]], [ftb])
                P.act(gacc[:], pb[4 + i][:], AF.Square, [pbb[4 + i]], [b_gacc, b_ssqp],
                      accum_out=ssqp[:, 32 + i * 4 + cg:32 + i * 4 + cg + 1])
        b_gbc = P.rebuf(b_gbc_0)
        b_junk3 = P.rebuf(b_junk3_0)
        P.dma("pool", gbc[:], gpost_d[1], writes=[b_gbc], sem=b_gbc.sem)
        for i in range(4):
            P.op("dve", lambda e, i=i: e.reduce_sum(out=st2[:, 32 + i:33 + i], in_=ssqp[:, 32 + i * 4:32 + i * 4 + 4],
                                                     axis=mybir.AxisListType.X), [b_ssqp], [b_st2])
        rstd_from_ssq("dve", st2[:, 36:40], st2[:, 32:36], D, [b_st2], [b_st2])
        for i in range(4):
            tt = g * 4 + i
            fti, ftb = ft[i]
            P.stt("dve", fti[:], fti[:], st2[:, 36 + i:37 + i], gbc[:], ALU.mult, ALU.mult, [ftb, b_st2, b_gbc], [ftb])
            P.dma("pool", out_d[tt * 128:(tt + 1) * 128, :], fti[:], reads=[ftb], writes=[b_outd[i]],
                  sem=b_outd[i].sem, accum=ALU.add)
    P.emit(final_bufs)
    return nc


def host_layout(inputs):
    f = np.float32
    w_in = np.asarray(inputs["w_in"][0], f)
    cq = w_in[:, 0:768]; ckv = w_in[:, 768:1280]; kr = w_in[:, 1280:1344]
    z = w_in[:, 1344:2368]; xbc = w_in[:, 2368:3904]; dt = w_in[:, 3904:3920]
    z64 = np.zeros((D, 64), f)
    kr_sw = np.concatenate([kr[:, 32:64], kr[:, 0:32]], axis=1)
    w_in_p = np.ascontiguousarray(np.concatenate([cq, ckv, kr, z64, kr_sw, z64, xbc, z, dt], axis=1))
    assert w_in_p.shape[1] == WIN_COLS
    w_uq = np.asarray(inputs["w_uq"][0], f)
    zq = np.zeros((768, 64), f)
    parts = []
    for h in range(8):
        nope = w_uq[:, h * 192:h * 192 + 128]
        rope = w_uq[:, h * 192 + 128:h * 192 + 192]
        rsw = np.concatenate([rope[:, 32:64], rope[:, 0:32]], axis=1)
        parts += [nope, rope, zq, rsw, zq]
    w_uq_p = np.ascontiguousarray(np.concatenate(parts, axis=1))
    w_ukv = np.asarray(inputs["w_ukv"][0], f)
    w_kn_p = np.ascontiguousarray(np.concatenate([w_ukv[:, h * 256:h * 256 + 128] for h in range(8)], axis=1))
    w_v_p = np.ascontiguousarray(np.concatenate([w_ukv[:, h * 256 + 128:h * 256 + 256] for h in range(8)], axis=1))

    def pm(v, n):
        return np.asarray(v, f).reshape(n, 128).T

    cw = np.asarray(inputs["ssm_conv_w"][0], f)
    fw = np.asarray(inputs["ffn_conv_w"][0], f)
    cvec = np.concatenate([
        pm(inputs["mix_pre_g"][0], 16), pm(inputs["q_norm_g"][0], 6), pm(inputs["kv_norm_g"][0], 4),
        cw.reshape(4, 12, 128).transpose(2, 1, 0).reshape(128, 48),
        pm(inputs["ssm_conv_b"][0], 12), pm(inputs["ffn_pre_g"][0], 16),
        fw.reshape(3, NJ, 128).transpose(2, 1, 0).reshape(128, NJ * 3),
        pm(inputs["ffn_conv_b"][0], NJ)], axis=1)
    cvec = np.ascontiguousarray(cvec, f)
    assert cvec.shape == (128, 278)
    row = np.concatenate([np.asarray(inputs["dt_bias"][0], f), np.asarray(inputs["a_log"][0], f),
                          np.asarray(inputs["d_skip"][0], f), np.asarray(inputs["ssm_norm_g"][0], f)])
    cbc = np.ascontiguousarray(np.broadcast_to(row[None, :], (128, row.size)), f)
    gpost = np.ascontiguousarray(np.stack([
        np.broadcast_to(np.asarray(inputs["mix_post_g"][0], f)[None, :], (128, D)),
        np.broadcast_to(np.asarray(inputs["ffn_post_g"][0], f)[None, :], (128, D))]), f)
    inv = 1.0 / (10000.0 ** (np.arange(0, 64, 2, dtype=np.float32) / 64.0))
    ang = np.arange(S, dtype=np.float32)[:, None] * inv[None, :].astype(np.float32)
    cos = np.cos(ang).T.astype(f); sin = np.sin(ang).T.astype(f)
    tab = np.zeros((2, 128, S), f)
    tab[0, 0:32] = cos; tab[0, 32:64] = cos
    tab[1, 0:32] = -sin; tab[1, 32:64] = sin
    k = np.arange(128)
    masks = np.zeros((128, 4, 128), f)
    masks[:, 0, :] = np.eye(128)
    masks[:, 1, :] = 1.0
    masks[:, 2, :] = (k[:, None] > k[None, :])
    masks[:, 3, :] = (k[:, None] <= k[None, :])
    shared = {
        "w_in_p": w_in_p, "w_uq_p": w_uq_p, "w_kn_p": w_kn_p, "w_v_p": w_v_p,
        "w_out_b": np.ascontiguousarray(np.asarray(inputs["w_out"][0], f).reshape(16, 128, 8, 256)
                                        .transpose(2, 1, 0, 3).reshape(8, 128, 4096)),
        "w_gate_b": np.ascontiguousarray(np.asarray(inputs["w_gate"][0], f).reshape(16, 128, NJ, 128)
                                         .transpose(2, 1, 0, 3).reshape(NJ, 128, 2048)),
        "w_up_b": np.ascontiguousarray(np.asarray(inputs["w_up"][0], f).reshape(16, 128, NJ, 128)
                                       .transpose(2, 1, 0, 3).reshape(NJ, 128, 2048)),
        "w_down_b": np.ascontiguousarray(np.asarray(inputs["w_down"][0], f).reshape(11, 4, 128, 4, 512)
                                         .transpose(3, 0, 2, 1, 4).reshape(44, 128, 2048)),
        "cvec": cvec, "cbc": cbc, "gpost": gpost, "ropetab": tab, "masks": masks,
    }
    return shared


def kernel(**inputs):
    x = np.asarray(inputs["x"], np.float32)
    shared = host_layout(inputs)
    nc = build_program()
    in_maps = []
    for c in range(8):
        m = dict(shared)
        m["x"] = np.ascontiguousarray(x[c])
        in_maps.append(m)
    res = run_bass_kernel_spmd(nc, in_maps, core_ids=list(range(8)))
    return np.stack([np.asarray(r["out"], np.float32) for r in res.results], axis=0)
```

```python
import numpy as np
import ml_dtypes
import concourse.bass as bass
import concourse.mybir as mybir
from concourse.bass_utils import run_bass_kernel_spmd

F32 = mybir.dt.float32
BF16 = mybir.dt.bfloat16
AF = mybir.ActivationFunctionType
ALU = mybir.AluOpType

S = 2048
D = 2048
NT = 16
KC = 16
EPS = 1e-6
DFF = 5632
NJ = 44
SB_BASE = 16512
SB_END = 229376
WIN_COLS = 4112
SCALE = 192.0 ** -0.5


class Buf:
    __slots__ = ("name", "w", "r", "sem", "rng", "excl")

    def __init__(self, name):
        self.name = name
        self.w = {}
        self.r = {}
        self.sem = None
        self.rng = None
        self.excl = False


class Prog:
    CE = ("pe", "act", "dve", "pool")

    def __init__(self, nc):
        self.nc = nc
        self.stream = {e: [] for e in ("pe", "act", "dve", "pool", "sp")}
        self.tick = {e: 0 for e in self.CE}
        self.pending = {e: False for e in self.CE}
        self.seen = {e: {} for e in self.stream}
        self.semval = {}
        self.sems = {}
        self.regions = []
        self.off = SB_BASE
        self.nbuf = 0

    def sem_for(self, key):
        if key not in self.sems:
            self.sems[key] = self.nc.alloc_semaphore(name="s_" + key)
            self.semval[key] = 0
        return self.sems[key]

    def newbuf(self, name, rng=None, dma=False, sem=None):
        b = Buf(name)
        self.nbuf += 1
        if sem is not None:
            b.sem = sem
        elif dma:
            b.sem = "d%d_%s" % (self.nbuf, name)
            self.sem_for(b.sem)
        if rng is not None:
            b.rng = rng
            for (o, e, ob) in self.regions:
                if o < rng[1] and rng[0] < e:
                    for k, v in ob.w.items():
                        b.r[k] = max(b.r.get(k, 0), v)
                    for k, v in ob.r.items():
                        b.r[k] = max(b.r.get(k, 0), v)
            self.regions.append((rng[0], rng[1], b))
        return b

    def alloc(self, name, shape, dtype, off, dma=False):
        nbytes = int(np.prod(shape[1:])) * (2 if dtype == BF16 else 4)
        assert off >= SB_BASE and off + nbytes <= SB_END, (name, off, nbytes)
        t = self.nc.alloc_sbuf_tensor_at(name, list(shape), dtype, offset=off)
        b = self.newbuf(name, (off, off + nbytes), dma=dma)
        return t, b

    def sub(self, parent, name, dma=False):
        return self.newbuf(name, parent.rng, dma=dma)

    def rebuf(self, old):
        return self.newbuf(old.name, old.rng, sem=old.sem)

    def _deps(self, eng, reads, writes):
        need = {}
        raw = set()
        for b in reads:
            for k, v in b.w.items():
                need[k] = max(need.get(k, 0), v)
                raw.add(k)
            if b.excl:
                for k, v in b.r.items():
                    if k != eng:
                        need[k] = max(need.get(k, 0), v)
        for b in writes:
            for k, v in b.w.items():
                need[k] = max(need.get(k, 0), v)
            for k, v in b.r.items():
                need[k] = max(need.get(k, 0), v)
        st = self.stream[eng]
        seen = self.seen[eng]
        for k, v in need.items():
            if k == eng:
                if k in raw and eng != "pe" and seen.get(k, 0) < v:
                    assert v <= self.tick[k], ("unsignaled same-engine dep", eng, k, v)
                    st.append(("w", k, v))
                    seen[k] = v
                continue
            if seen.get(k, 0) >= v:
                continue
            if k in self.tick:
                assert v <= self.tick[k], ("dep on unsignaled op", eng, k, v, self.tick[k])
            st.append(("w", k, v))
            seen[k] = v

    def op(self, eng, fn, reads=(), writes=(), signal=True):
        self._deps(eng, reads, writes)
        if signal:
            self.tick[eng] += 1
            ev = self.tick[eng]
            self.pending[eng] = False
        else:
            ev = self.tick[eng] + 1
            self.pending[eng] = True
        self.stream[eng].append(("o", fn, signal))
        for b in writes:
            b.w[eng] = max(b.w.get(eng, 0), ev)
        for b in reads:
            b.r[eng] = max(b.r.get(eng, 0), ev)

    def dma(self, q, out, in_, reads=(), writes=(), sem=None, accum=None):
        assert sem is not None
        self.sem_for(sem)
        self._deps(q, reads, writes)
        self.semval[sem] += 16
        v = self.semval[sem]
        self.stream[q].append(("d", out, in_, sem, accum))
        for b in writes:
            b.w[sem] = max(b.w.get(sem, 0), v)
        for b in reads:
            b.r[sem] = max(b.r.get(sem, 0), v)

    def wait_all(self, eng, bufs):
        self._deps(eng, bufs, ())

    def mm(self, out, lhsT, rhs, start, stop, reads, writes, signal=False):
        self.op("pe", lambda e: e.matmul(out, lhsT=lhsT, rhs=rhs, start=start, stop=stop),
                reads, writes, signal)

    def tr(self, out, in_, ident, reads, writes, signal=False):
        self.op("pe", lambda e: e.transpose(out, in_, ident), reads, writes, signal)

    def act(self, out, in_, func, reads, writes, bias=None, scale=None, accum_out=None):
        kw = {}
        if bias is not None:
            kw["bias"] = bias
        if scale is not None:
            kw["scale"] = scale
        if accum_out is not None:
            kw["accum_out"] = accum_out
        self.op("act", lambda e: e.activation(out=out, in_=in_, func=func, **kw), reads, writes)

    def tt(self, eng, out, in0, in1, op, reads, writes):
        self.op(eng, lambda e: e.tensor_tensor(out=out, in0=in0, in1=in1, op=op), reads, writes)

    def ts(self, eng, out, in0, s1, s2, op0, op1, reads, writes):
        if op1 is None:
            self.op(eng, lambda e: e.tensor_scalar(out=out, in0=in0, scalar1=s1, scalar2=None, op0=op0),
                    reads, writes)
        else:
            self.op(eng, lambda e: e.tensor_scalar(out=out, in0=in0, scalar1=s1, scalar2=s2, op0=op0, op1=op1),
                    reads, writes)

    def stt(self, eng, out, in0, scalar, in1, op0, op1, reads, writes):
        self.op(eng, lambda e: e.scalar_tensor_tensor(out=out, in0=in0, scalar=scalar, in1=in1, op0=op0, op1=op1),
                reads, writes)

    def copy(self, eng, out, in_, reads, writes):
        if eng == "act":
            self.op("act", lambda e: e.copy(out=out, in_=in_), reads, writes)
        else:
            self.op(eng, lambda e: e.tensor_copy(out=out, in_=in_), reads, writes)

    def memset(self, eng, ap, val, writes):
        self.op(eng, lambda e: e.memset(ap, val), (), writes)

    def recip(self, out, in_, reads, writes):
        self.op("dve", lambda e: e.reciprocal(out=out, in_=in_), reads, writes)

    def emit(self, final_bufs):
        nc = self.nc
        self._deps("sp", final_bufs, final_bufs)
        for e in self.CE:
            self.sem_for(e)
        sems = self.sems
        streams = self.stream

        def replay(eng_name):
            def run(eng):
                for it in streams[eng_name]:
                    if it[0] == "w":
                        eng.wait_ge(sems[it[1]], it[2])
                    elif it[0] == "o":
                        ins = it[1](eng)
                        if it[2]:
                            ins.then_inc(sems[eng_name], 1)
                    else:
                        if it[4] is None:
                            eng.dma_start(out=it[1], in_=it[2]).then_inc(sems[it[3]], 16)
                        else:
                            eng.dma_start(out=it[1], in_=it[2], accum_op=it[4]).then_inc(sems[it[3]], 16)
            return run

        with nc.Block() as block:
            block.tensor(replay("pe"))
            block.scalar(replay("act"))
            block.vector(replay("dve"))
            block.gpsimd(replay("pool"))
            block.sync(replay("sp"))


def bc(ap, shape):
    return ap.to_broadcast(list(shape))


def build_program(debug=None, stop_after=None):
    nc = bass.Bass("TRN2", target_bir_lowering=False)
    P = Prog(nc)
    KB = 1024

    def din(name, shape):
        return nc.dram_tensor(name, list(shape), F32, kind="ExternalInput").ap()

    x_d = din("x", [S, D])
    win_d = din("w_in_p", [D, WIN_COLS])
    wuq_d = din("w_uq_p", [768, 8 * 384])
    wkn_d = din("w_kn_p", [512, 1024])
    wv_d = din("w_v_p", [512, 1024])
    wout_d = din("w_out_b", [8, 128, 4096])
    wgate_d = din("w_gate_b", [NJ, 128, 2048])
    wup_d = din("w_up_b", [NJ, 128, 2048])
    wdown_d = din("w_down_b", [4 * 11, 128, 2048])

    def dscr(name, shape):
        return nc.dram_tensor(name, list(shape), BF16, kind="Internal").ap()

    wout_s = dscr("w_out_s", [8, 128, 4096])
    wgate_s = dscr("w_gate_s", [NJ, 128, 2048])
    wup_s = dscr("w_up_s", [NJ, 128, 2048])
    wdown_s = dscr("w_down_s", [4 * 11, 128, 2048])
    cvec_d = din("cvec", [128, 278])
    cbc_d = din("cbc", [128, 48 + 1024])
    gpost_d = din("gpost", [2, 128, 2048])
    tab_d = din("ropetab", [2, 128, S])
    mask_d = din("masks", [128, 4, 128])
    out_d = nc.dram_tensor("out", [S, D], F32, kind="ExternalOutput").ap()
    dbg_out = {}
    if debug:
        for name, shape in debug.items():
            dbg_out[name] = nc.dram_tensor("dbg_" + name, list(shape), F32, kind="ExternalOutput").ap()

    O_CONST = SB_BASE
    O_HT = O_CONST + 13312
    O_XS = O_HT + 64 * KB
    O_AO = O_XS + 32 * KB
    O_R3 = O_AO + 32 * KB
    O_R4 = O_R3 + 48 * KB

    o = O_CONST
    cvec, b_cvec = P.alloc("cvec", [128, 278], F32, o, dma=True); o += 1120
    cbc, b_cbc = P.alloc("cbc", [128, 48 + 1024], F32, o, dma=True); o += (48 + 1024) * 4
    masks, b_masks = P.alloc("masks", [128, 4, 128], F32, o, dma=True); o += 2048
    mbf, b_mbf = P.alloc("mbf", [128, 4, 128], BF16, o, dma=True); o += 1024
    aah, b_aah = P.alloc("aah", [128, 16, 16], BF16, o); o += 512
    aal, b_aal = P.alloc("aal", [128, 16, 16], BF16, o); o += 512
    aalf, b_aalf = P.alloc("aalf", [128, 16, 16], F32, o); o += 1024
    stat, b_stat = P.alloc("stat", [128, 64], F32, o); o += 256
    dtt, b_dtt = P.alloc("dtt", [128, 16, 16], F32, o); o += 1024
    aa, b_aa = P.alloc("aa", [128, 16, 16], F32, o); o += 1024
    abc, b_abc = P.alloc("abc", [128, 16], F32, o); o += 64
    halo, b_halo = P.alloc("halo", [128, NJ, 2], F32, o); o += 352
    assert o <= O_HT, o
    ident_f = masks[:, 0, :]
    ones_f = masks[:, 1, :]
    Um = masks[:, 2, :]
    Tm = masks[:, 3, :]
    ident_b = mbf[:, 0, :]
    ones_b = mbf[:, 1, :]
    Um_b = mbf[:, 2, :]
    Tm_b = mbf[:, 3, :]
    C_GPRE, C_GQ, C_GKV, C_CW, C_CB, C_GF, C_FW, C_FB = 0, 16, 22, 26, 74, 86, 102, 234
    B_DTB, B_ALOG, B_DSK, B_NG = 0, 16, 32, 48

    P.dma("sp", cvec[:], cvec_d, writes=[b_cvec], sem=b_cvec.sem)
    P.dma("sp", cbc[:], cbc_d, writes=[b_cbc], sem=b_cbc.sem)
    P.dma("sp", masks[:], mask_d, writes=[b_masks], sem=b_masks.sem)
    P.dma("pool", mbf[:], mask_d, writes=[b_mbf], sem=b_mbf.sem)

    pb = []
    pbb = []
    for i in range(8):
        t = nc.alloc_psum_tensor("pb%d" % i, [128, 512], F32)
        pb.append(t)
        pbb.append(P.newbuf("pb%d" % i))
        pbb[-1].excl = True

    b_scr = P.newbuf("scratch")
    conv_jobs = [(wout_s[i], wout_d[i]) for i in range(8)]
    for j in range(NJ):
        conv_jobs.append((wgate_s[j], wgate_d[j]))
        conv_jobs.append((wup_s[j], wup_d[j]))
    conv_jobs += [(wdown_s[i], wdown_d[i]) for i in range(44)]
    conv_pos = [0]

    def conv_some(n):
        for _ in range(n):
            if conv_pos[0] < len(conv_jobs):
                o_, i_ = conv_jobs[conv_pos[0]]
                conv_pos[0] += 1
                P.dma("pool", o_, i_, writes=[b_scr], sem="conv")

    def rstd_from_ssq(eng, out, in_, n, reads, writes):
        P.ts("dve", out, in_, 1.0 / n, EPS, ALU.mult, ALU.add, reads, writes)
        P.act(out, out, AF.Ln, writes, writes)
        P.act(out, out, AF.Exp, writes, writes, scale=-0.5)

    hT, b_hT_all = P.alloc("hT", [128, KC, S], BF16, O_HT)
    b_hT = [P.sub(b_hT_all, "hT%d" % i) for i in range(NT)]
    xsl = []
    for i in range(2):
        t, b = P.alloc("xsl%d" % i, [128, D], F32, O_R3 + i * 8 * KB, dma=True)
        xsl.append((t, b))
    xn = []
    for i in range(2):
        t, b = P.alloc("xn%d" % i, [128, D], BF16, O_R3 + 16 * KB + i * 4 * KB)
        xn.append((t, b))
    junk, b_junk = P.alloc("junk", [128, D], BF16, O_R3 + 24 * KB)

    def emit_a0(tt):
        xt, xb = xsl[tt % 2]
        nt_, nb_ = xn[tt % 2]
        P.dma("sp", xt[:], x_d[tt * 128:(tt + 1) * 128, :], writes=[xb], sem=xb.sem)
        P.act(junk[:], xt[:], AF.Square, [xb], [b_junk, b_stat], accum_out=stat[:, 0:1])
        rstd_from_ssq("dve", stat[:, 1:2], stat[:, 0:1], D, [b_stat], [b_stat])
        P.act(nt_[:], xt[:], AF.Copy, [xb, b_stat], [nb_], scale=stat[:, 1:2])
        for q in range(4):
            bank = (tt * 4 + q) % 4
            pt = pb[bank][:].bitcast(BF16)
            for i in range(4):
                kc = q * 4 + i
                P.tr(pt[:, i * 128:(i + 1) * 128], nt_[:, kc * 128:(kc + 1) * 128], ident_b,
                     [nb_, b_mbf], [pbb[bank]], signal=(i == 3))
            P.tt("dve", hT[:, q * 4:(q + 1) * 4, tt * 128:(tt + 1) * 128],
                 pt[:, 0:512].rearrange("p (a b) -> p a b", a=4),
                 bc(cvec[:, C_GPRE + q * 4:C_GPRE + q * 4 + 4].unsqueeze(2), [128, 4, 128]),
                 ALU.mult, [pbb[bank], b_cvec], [b_hT[tt]])

    def dump(name, ap, bufs):
        if debug and name in dbg_out:
            b = P.newbuf("dbg_" + name, dma=True)
            P.dma("pool", dbg_out[name], ap, reads=bufs, writes=[b], sem=b.sem)
            final_bufs.append(b)

    final_bufs = []

    wsl = []
    for i in range(2):
        t, b = P.alloc("wsl%d" % i, [128, KC, 256], BF16, O_R4 + i * 8 * KB, dma=True)
        wsl.append((t, b))
    wctr = [0]

    def load_w(src_ap, kc, ncols):
        t, b = wsl[wctr[0] % 2]
        wctr[0] += 1
        P.dma("pool", t[:, 0:kc, 0:ncols], src_ap.rearrange("(kc p) c -> p kc c", p=128), writes=[b], sem=b.sem)
        conv_some(2)
        return t, b

    xsT, b_xsT_all = P.alloc("xsT", [128, 8, S], BF16, O_XS)
    b_xsT = [P.sub(b_xsT_all, "xsT%d" % i) for i in range(NT)]
    BCT, b_BCT_all = P.alloc("BCT", [128, 4, S], BF16, O_R3)
    b_BCT = [P.sub(b_BCT_all, "BCT%d" % i) for i in range(NT)]
    zs, b_zs_all = P.alloc("zs", [128, NT, 1024], BF16, O_R3 + 16 * KB)
    b_zs = [P.sub(b_zs_all, "zs%d" % i) for i in range(NT)]
    pcb = []
    for i in range(2):
        t, b = P.alloc("pcb%d" % i, [128, 3 + 512], F32, O_AO + i * 2080)
        pcb.append((t, b))
    cacc_l = [P.alloc("cacc0", [128, 512], F32, O_AO + 2 * 2080),
              P.alloc("cacc1", [128, 512], F32, O_AO + 2 * 2080 + 2048)]

    bankc = [0]

    def next_bank(lo=0, n=8):
        b = lo + bankc[0] % n
        bankc[0] += 1
        return b

    XBC0 = 1536
    hal, b_hal = P.alloc("hal", [128, 12, 3], F32, O_AO + 2 * 2080 + 2 * 2048)
    pcs = [0]
    pend_silu = [None]

    def xbc_unit(wt, wb, jb, blk, tg):
        bank = next_bank()
        for kc in range(KC):
            P.mm(pb[bank][:], wt[:, kc, jb * 128:(jb + 1) * 128], hT[:, kc, tg * 512:(tg + 1) * 512],
                 kc == 0, kc == KC - 1, [wb] + b_hT[tg * 4:tg * 4 + 4], [pbb[bank]], signal=(kc == KC - 1))
        pt_, pb_ = pcb[pcs[0] % 2]
        cacc, b_cacc = cacc_l[pcs[0] % 2]
        pcs[0] += 1
        if tg == 0:
            P.memset("dve", pt_[:, 0:3], 0.0, [pb_])
        else:
            P.copy("dve", pt_[:, 0:3], hal[:, blk, :], [b_hal], [pb_])
        P.copy("act", pt_[:, 3:515], pb[bank][:], [pbb[bank]], [pb_])
        if pend_silu[0] is not None:
            pend_silu[0]()
            pend_silu[0] = None
        if tg < 3:
            P.copy("dve", hal[:, blk, :], pt_[:, 512:515], [pb_], [b_hal])
        cw = C_CW + blk * 4
        P.ts("dve", cacc[:], pt_[:, 3:515], cvec[:, cw + 3:cw + 4], cvec[:, C_CB + blk:C_CB + blk + 1],
             ALU.mult, ALU.add, [pb_, b_cvec], [b_cacc])
        for j in range(3):
            P.stt("dve", cacc[:], pt_[:, j:j + 512], cvec[:, cw + j:cw + j + 1], cacc[:],
                  ALU.mult, ALU.add, [pb_, b_cacc, b_cvec], [b_cacc])
        if blk < 8:
            dst = xsT[:, blk, tg * 512:(tg + 1) * 512]
            wbufs = b_xsT[tg * 4:tg * 4 + 4]
        else:
            dst = BCT[:, blk - 8, tg * 512:(tg + 1) * 512]
            wbufs = b_BCT[tg * 4:tg * 4 + 4]
        pend_silu[0] = lambda: P.act(dst, cacc[:], AF.Silu, [b_cacc], wbufs)

    w01 = [load_w(win_d[:, XBC0 + grp * 256: XBC0 + (grp + 1) * 256], KC, 256) for grp in range(2)]
    for tt in range(4):
        emit_a0(tt)
    for tg in range(4):
        if tg + 1 < 4:
            for tt in range(4 * (tg + 1), 4 * (tg + 1) + 4):
                emit_a0(tt)
        for grp in range(2):
            for jb in range(2):
                xbc_unit(w01[grp][0], w01[grp][1], jb, grp * 2 + jb, tg)
    for grp in range(2, 6):
        wt, wb = load_w(win_d[:, XBC0 + grp * 256: XBC0 + (grp + 1) * 256], KC, 256)
        for jb in range(2):
            for tg in range(4):
                xbc_unit(wt, wb, jb, grp * 2 + jb, tg)
    pend_silu[0]()
    pend_silu[0] = None

    Z0 = 3072
    for grp in range(4):
        wt, wb = load_w(win_d[:, Z0 + grp * 256: Z0 + (grp + 1) * 256], KC, 256)
        for tt in range(NT):
            bank = next_bank()
            for kc in range(KC):
                P.mm(pb[bank][:, 0:256], hT[:, kc, tt * 128:(tt + 1) * 128], wt[:, kc, :],
                     kc == 0, kc == KC - 1, [wb, b_hT[tt]], [pbb[bank]], signal=(kc == KC - 1))
            P.act(zs[:, tt, grp * 256:(grp + 1) * 256], pb[bank][:, 0:256], AF.Silu, [pbb[bank]], [b_zs[tt]])
    wt, wb = load_w(win_d[:, 4096:4112], KC, 16)
    for tt in range(NT):
        bank = next_bank()
        for kc in range(KC):
            P.mm(pb[bank][:, 0:16], hT[:, kc, tt * 128:(tt + 1) * 128], wt[:, kc, 0:16],
                 kc == 0, kc == KC - 1, [wb, b_hT[tt]], [pbb[bank]], signal=(kc == KC - 1))
        P.tt("dve", dtt[:, tt, :], pb[bank][:, 0:16], cbc[:, B_DTB:B_DTB + 16], ALU.add, [pbb[bank], b_cbc], [b_dtt])
    P.act(dtt[:], dtt[:], AF.Exp, [b_dtt], [b_dtt])
    P.act(dtt[:], dtt[:], AF.Ln, [b_dtt], [b_dtt], bias=1.0)
    P.act(abc[:], cbc[:, B_ALOG:B_ALOG + 16], AF.Exp, [b_cbc], [b_abc])
    P.stt("dve", aa[:], dtt[:], -1.0, bc(abc[:].unsqueeze(1), [128, 16, 16]), ALU.mult, ALU.mult,
          [b_dtt, b_abc], [b_aa])
    P.copy("dve", aah[:], aa[:], [b_aa], [b_aah])
    P.tt("dve", aalf[:], aa[:], aah[:], ALU.subtract, [b_aa, b_aah], [b_aalf])
    P.copy("dve", aal[:], aalf[:], [b_aalf], [b_aal])

    dump("xsT", xsT[:, 0, :], b_xsT)
    dump("BCT", BCT[:, 0, :], b_BCT)
    dump("zs", zs[:, 0, :], b_zs)
    dump("dtt", dtt[:].rearrange("p a b -> p (a b)"), [b_dtt])
    dump("hT", hT[:, 0, :], b_hT)

    if stop_after == "A1":
        P.emit(final_bufs)
        return nc

    o = O_AO
    lallh, b_lall = P.alloc("lallh", [128, 16, 128], BF16, o); o += 4 * KB
    lalll, b_lalll = P.alloc("lalll", [128, 16, 128], BF16, o); o += 4 * KB
    Ef, b_Ef = P.alloc("Ef", [128, 16, 128], BF16, o); o += 4 * KB
    Wb_l = [P.alloc("Wb0", [128, 16, 128], BF16, o)]; o += 4 * KB
    Wb_l.append(P.alloc("Wb1", [128, 16, 128], BF16, o)); o += 4 * KB
    xstok_l = [P.alloc("xstok0", [128, 1024], BF16, o)]; o += 2 * KB
    xdt_l = [P.alloc("xdt0", [128, 1024], BF16, o)]; o += 2 * KB
    xdtd_l = [P.alloc("xdtd0", [128, 1024], BF16, o)]; o += 2 * KB
    Sst, b_Sst = P.alloc("Sst", [128, 1024], F32, o); o += 4 * KB
    Sbf, b_Sbf = P.alloc("Sbf", [128, 1024], BF16, o); o += 2 * KB
    assert o <= O_AO + 32 * KB
    o = O_R4 + 0
    ycomb, b_ycomb = P.alloc("ycomb", [128, 1024], F32, o); o += 4 * KB
    ytmp, b_ytmp = P.alloc("ytmp", [128, 1024], F32, o); o += 4 * KB
    btok, b_btok = P.alloc("btok", [128, 1024], BF16, o); o += 2 * KB
    Btok_l = [P.alloc("Btok0", [128, 256], BF16, o)]; o += 512
    Btok_l.append(P.alloc("Btok1", [128, 256], BF16, o)); o += 512
    Gm, b_Gm = P.alloc("Gm", [128, 2, 128], F32, o); o += 1024
    eacs_l = [P.alloc("eacs0", [128, 32], F32, o)]; o += 128
    eacs_l.append(P.alloc("eacs1", [128, 32], F32, o)); o += 128
    sq2, b_sq2 = P.alloc("sq2", [128, 4], F32, o); o += 32
    xstok_l.append(P.alloc("xstok1", [128, 1024], BF16, o)); o += 2 * KB
    xdt_l.append(P.alloc("xdt1", [128, 1024], BF16, o)); o += 2 * KB
    xdtd_l.append(P.alloc("xdtd1", [128, 1024], BF16, o)); o += 2 * KB
    assert o <= SB_END, o
    boutT = xsT
    b_boutT = b_xsT
    ngb = cbc[:, B_NG:B_NG + 1024]
    import os
    NCH = int(os.environ.get('NCH', NT))

    def ssd_prep(c):
        cs = slice(c * 128, (c + 1) * 128)
        conv_some(3)
        xstok, b_xstok = xstok_l[c % 2]
        xdt, b_xdt = xdt_l[c % 2]
        xdtd, b_xdtd = xdtd_l[c % 2]
        Wb, b_Wb = Wb_l[c % 2]
        Btok, b_Btok = Btok_l[c % 2]
        eacs, b_eacs = eacs_l[c % 2]
        bk = next_bank()
        pt = pb[bk][:].bitcast(BF16)
        for j in range(8):
            P.tr(pt[:, j * 128:(j + 1) * 128], xsT[:, j, cs], ident_b, [b_xsT[c], b_mbf], [pbb[bk]], signal=(j == 7))
        P.copy("act", xstok[:], pt[:, 0:1024], [pbb[bk]], [b_xstok])
        P.tt("dve", xdt[:].rearrange("p (h q) -> p h q", h=16), pt[:, 0:1024].rearrange("p (h q) -> p h q", h=16),
             bc(dtt[:, c, :].unsqueeze(2), [128, 16, 64]), ALU.mult, [pbb[bk], b_dtt], [b_xdt])
        bk2 = next_bank()
        pt2 = pb[bk2][:].bitcast(BF16)
        for g in range(2):
            P.tr(pt2[:, g * 128:(g + 1) * 128], BCT[:, g, cs], ident_b, [b_BCT[c], b_mbf], [pbb[bk2]], signal=(g == 1))
        P.copy("act", Btok[:], pt2[:, 0:256], [pbb[bk2]], [b_Btok])
        bk3 = next_bank()
        for g in range(2):
            P.mm(pb[bk3][:, g * 128:(g + 1) * 128], BCT[:, g, cs], BCT[:, 2 + g, cs], True, True,
                 [b_BCT[c]], [pbb[bk3]], signal=(g == 1))
        P.tt("dve", Gm[:], pb[bk3][:, 0:256].rearrange("p (g l) -> p g l", g=2),
             bc(Tm.unsqueeze(1), [128, 2, 128]), ALU.mult, [pbb[bk3], b_masks], [b_Gm])
    def ssd_prep_a2(c):
        P.tt("pool", lallh[:], bc(Um_b.unsqueeze(1), [128, 16, 128]), bc(aah[:, c, :].unsqueeze(2), [128, 16, 128]),
             ALU.mult, [b_mbf, b_aah], [b_lall])
        P.tt("pool", lalll[:], bc(Um_b.unsqueeze(1), [128, 16, 128]), bc(aal[:, c, :].unsqueeze(2), [128, 16, 128]),
             ALU.mult, [b_mbf, b_aal], [b_lalll])
        for q in range(4):
            bk4 = next_bank()
            for i in range(4):
                h = q * 4 + i
                P.mm(pb[bk4][:, i * 128:(i + 1) * 128], lallh[:, h, :], Tm_b, True, False,
                     [b_lall, b_mbf], [pbb[bk4]], signal=False)
                P.mm(pb[bk4][:, i * 128:(i + 1) * 128], lalll[:, h, :], Tm_b, False, True,
                     [b_lalll, b_mbf], [pbb[bk4]], signal=(i == 3))
            P.act(Ef[:, q * 4:(q + 1) * 4, :], pb[bk4][:].rearrange("p (a b) -> p a b", a=4), AF.Exp,
                  [pbb[bk4]], [b_Ef])
    def ssd_prep_b(c):
        xdt, b_xdt = xdt_l[c % 2]
        xdtd, b_xdtd = xdtd_l[c % 2]
        Wb, b_Wb = Wb_l[c % 2]
        eacs, b_eacs = eacs_l[c % 2]
        for g in range(2):
            P.tt("dve", Wb[:, g * 8:(g + 1) * 8, :], Ef[:, g * 8:(g + 1) * 8, :],
                 bc(Gm[:, g, :].unsqueeze(1), [128, 8, 128]), ALU.mult, [b_Ef, b_Gm], [b_Wb])
        bk5 = next_bank()
        P.mm(pb[bk5][:, 0:16], Tm_b, aah[:, c, :], True, False, [b_mbf, b_aah], [pbb[bk5]], signal=False)
        P.mm(pb[bk5][:, 0:16], Tm_b, aal[:, c, :], False, True, [b_mbf, b_aal], [pbb[bk5]], signal=False)
        P.mm(pb[bk5][:, 16:32], ones_b, aah[:, c, :], True, False, [b_mbf, b_aah], [pbb[bk5]], signal=False)
        P.mm(pb[bk5][:, 16:32], ones_b, aal[:, c, :], False, True, [b_mbf, b_aal], [pbb[bk5]], signal=True)
        P.act(eacs[:], pb[bk5][:, 0:32], AF.Exp, [pbb[bk5]], [b_eacs])
        P.tt("dve", xdtd[:].rearrange("p (h q) -> p h q", h=16), xdt[:].rearrange("p (h q) -> p h q", h=16),
             bc(Ef[:, :, 127:128], [128, 16, 64]), ALU.mult, [b_xdt, b_Ef], [b_xdtd])

    def ssd_finish(c):
        cs = slice(c * 128, (c + 1) * 128)
        xstok, b_xstok = xstok_l[c % 2]
        xdt, b_xdt = xdt_l[c % 2]
        xdtd, b_xdtd = xdtd_l[c % 2]
        Wb, b_Wb = Wb_l[c % 2]
        Btok, b_Btok = Btok_l[c % 2]
        eacs, b_eacs = eacs_l[c % 2]
        bko = [next_bank(), next_bank()]
        bkd = [next_bank(), next_bank()]
        if c > 0:
            for g in range(2):
                P.mm(pb[bko[g]][:], BCT[:, 2 + g, cs], Sbf[:, g * 512:(g + 1) * 512], True, True,
                     [b_BCT[c], b_Sbf], [pbb[bko[g]]], signal=True)
        for g in range(2):
            for i in range(8):
                h = g * 8 + i
                P.mm(pb[bkd[g]][:, i * 64:(i + 1) * 64], Wb[:, h, :], xdt[:, h * 64:(h + 1) * 64], True, True,
                     [b_Wb, b_xdt], [pbb[bkd[g]]], signal=(i == 7))
        fin_state[c] = (bko, bkd)

    def ssd_finish_ew(c):
        xstok, b_xstok = xstok_l[c % 2]
        eacs, b_eacs = eacs_l[c % 2]
        bko, bkd = fin_state[c]
        for g in range(2):
            hs = slice(g * 512, (g + 1) * 512)
            if c > 0:
                P.tt("dve", ycomb[:, hs].rearrange("p (h q) -> p h q", h=8),
                     pb[bko[g]][:].rearrange("p (h q) -> p h q", h=8),
                     bc(eacs[:, g * 8:(g + 1) * 8].unsqueeze(2), [128, 8, 64]), ALU.mult,
                     [pbb[bko[g]], b_eacs], [b_ycomb])
                P.tt("dve", ycomb[:, hs], ycomb[:, hs], pb[bkd[g]][:], ALU.add, [b_ycomb, pbb[bkd[g]]], [b_ycomb])
            else:
                P.copy("dve", ycomb[:, hs], pb[bkd[g]][:], [pbb[bkd[g]]], [b_ycomb])
        P.tt("dve", ytmp[:].rearrange("p (h q) -> p h q", h=16), xstok[:].rearrange("p (h q) -> p h q", h=16),
             bc(cbc[:, B_DSK:B_DSK + 16].unsqueeze(2), [128, 16, 64]), ALU.mult, [b_xstok, b_cbc], [b_ytmp])
        P.tt("dve", ycomb[:], ycomb[:], ytmp[:], ALU.add, [b_ycomb, b_ytmp], [b_ycomb])
        P.tt("dve", ycomb[:], ycomb[:], zs[:, c, :], ALU.mult, [b_ycomb, b_zs[c]], [b_ycomb])
        for g in range(2):
            P.act(btok[:, g * 512:(g + 1) * 512], ycomb[:, g * 512:(g + 1) * 512], AF.Square, [b_ycomb],
                  [b_btok, b_sq2], accum_out=sq2[:, g:g + 1])
    def ssd_finish_b(c):
        cs = slice(c * 128, (c + 1) * 128)
        xdtd, b_xdtd = xdtd_l[c % 2]
        Btok, b_Btok = Btok_l[c % 2]
        eacs, b_eacs = eacs_l[c % 2]
        rstd_from_ssq("dve", sq2[:, 2:4], sq2[:, 0:2], 512, [b_sq2], [b_sq2])
        for g in range(2):
            P.stt("dve", btok[:, g * 512:(g + 1) * 512], ycomb[:, g * 512:(g + 1) * 512], sq2[:, 2 + g:3 + g],
                  ngb[:, g * 512:(g + 1) * 512], ALU.mult, ALU.mult, [b_ycomb, b_sq2, b_cbc], [b_btok])
        bk6 = next_bank()
        pt6 = pb[bk6][:].bitcast(BF16)
        for j in range(8):
            P.tr(pt6[:, j * 128:(j + 1) * 128], btok[:, j * 128:(j + 1) * 128], ident_b, [b_btok, b_mbf], [pbb[bk6]],
                 signal=(j == 7))
        P.copy("act", boutT[:, :, cs], pt6[:, 0:1024].rearrange("p (j t) -> p j t", j=8), [pbb[bk6]], [b_boutT[c]])
        if c < NT - 1:
            bks = [next_bank(), next_bank()]
            for g in range(2):
                P.mm(pb[bks[g]][:], Btok[:, g * 128:(g + 1) * 128], xdtd[:, g * 512:(g + 1) * 512], True, True,
                     [b_Btok, b_xdtd], [pbb[bks[g]]], signal=True)
            P.tt("dve", Sst[:].rearrange("p (h q) -> p h q", h=16), Sst[:].rearrange("p (h q) -> p h q", h=16),
                 bc(eacs[:, 16:32].unsqueeze(2), [128, 16, 64]), ALU.mult, [b_Sst, b_eacs], [b_Sst])
            for g in range(2):
                P.tt("dve", Sst[:, g * 512:(g + 1) * 512], Sst[:, g * 512:(g + 1) * 512], pb[bks[g]][:], ALU.add,
                     [b_Sst, pbb[bks[g]]], [b_Sst])
            P.copy("act", Sbf[:], Sst[:], [b_Sst], [b_Sbf])

    P.memset("dve", Sst[:], 0.0, [b_Sst])
    fin_state = {}
    ssd_prep(0)
    ssd_prep_a2(0)
    ssd_prep_b(0)
    for c in range(NCH):
        if c + 1 < NCH:
            ssd_prep(c + 1)
        ssd_finish(c)
        if c + 1 < NCH:
            ssd_prep_a2(c + 1)
        ssd_finish_ew(c)
        if c + 1 < NCH:
            ssd_prep_b(c + 1)
        ssd_finish_b(c)

    dump("boutT", boutT[:, 0, :], b_boutT)
    dump("boutT7", boutT[:, 7, :], b_boutT)
    if stop_after == "A2":
        P.emit(final_bufs)
        return nc

    cqn, b_cqn = P.alloc("cqn", [128, 6, S], BF16, O_R3)
    ckvn, b_ckvn = P.alloc("ckvn", [128, 4, S], BF16, O_R3 + 24 * KB)
    krT, b_krT = P.alloc("krT", [128, S], BF16, O_R3 + 40 * KB)
    wsl2 = []
    for i in range(2):
        t, b = P.alloc("wslb%d" % i, [128, KC, 256], BF16, O_R4 + i * 8 * KB, dma=True)
        wsl2.append((t, b))
    sqt = []
    for i in range(3):
        t, b = P.alloc("sqt%d" % i, [128, 512], BF16, O_AO + 20 * KB + i * KB)
        sqt.append((t, b))
    rsb, b_rsb = P.alloc("rsb", [128, 512], F32, O_AO + 18 * KB)
    tabA, b_tabA = P.alloc("tabA", [128, 2, S], F32, O_AO, dma=True)
    P.dma("sp", tabA[:, 0, :], tab_d[0], writes=[b_tabA], sem=b_tabA.sem)
    P.dma("sp", tabA[:, 1, :], tab_d[1], writes=[b_tabA], sem=b_tabA.sem)

    w2c = [0]

    def load_w2(src_ap, kc, ncols):
        t, b = wsl2[w2c[0] % 2]
        w2c[0] += 1
        P.dma("pool", t[:, 0:kc, 0:ncols], src_ap.rearrange("(kc p) c -> p kc c", p=128), writes=[b], sem=b.sem)
        conv_some(2)
        return t, b

    sqc = 0
    pend_ones = [None]
    for (dst, dbuf, col0, nblk, gcol, nfeat) in ((cqn, b_cqn, 0, 6, C_GQ, 768), (ckvn, b_ckvn, 768, 4, C_GKV, 512)):
        for grp in range(nblk // 2):
            wt, wb = load_w2(win_d[:, col0 + grp * 256: col0 + (grp + 1) * 256], KC, 256)
            for jb in range(2):
                blk = grp * 2 + jb
                for tg in range(4):
                    bank = next_bank(0, 4)
                    for kc in range(KC):
                        P.mm(pb[bank][:], wt[:, kc, jb * 128:(jb + 1) * 128], hT[:, kc, tg * 512:(tg + 1) * 512],
                             kc == 0, kc == KC - 1, [wb] + b_hT[tg * 4:tg * 4 + 4], [pbb[bank]], signal=(kc == KC - 1))
                    st_, sb_ = sqt[sqc % 3]
                    sqc += 1
                    P.act(st_[:], pb[bank][:], AF.Square, [pbb[bank]], [sb_])
                    P.ts("dve", dst[:, blk, tg * 512:(tg + 1) * 512], pb[bank][:], cvec[:, gcol + blk:gcol + blk + 1],
                         None, ALU.mult, None, [pbb[bank], b_cvec], [dbuf])
                    if pend_ones[0] is not None:
                        pend_ones[0]()
                    pend_ones[0] = (lambda tg=tg, st_=st_, sb_=sb_, blk=blk, nblk=nblk:
                                    P.mm(pb[4 + tg][:], ones_b, st_[:], blk == 0, blk == nblk - 1, [sb_, b_mbf],
                                         [pbb[4 + tg]], signal=True))
        pend_ones[0]()
        pend_ones[0] = None
        for tg in range(4):
            rstd_from_ssq("dve", rsb[:], pb[4 + tg][:], nfeat, [pbb[4 + tg]], [b_rsb])
            P.tt("dve", dst[:, :, tg * 512:(tg + 1) * 512], dst[:, :, tg * 512:(tg + 1) * 512],
                 bc(rsb[:].unsqueeze(1), [128, nblk, 512]), ALU.mult, [dbuf, b_rsb], [dbuf])
    wt, wb = load_w2(win_d[:, 1280:1536], KC, 256)
    rt1, b_rt1 = P.alloc("rt1", [128, 512], F32, O_AO + 16 * KB)
    for tg in range(4):
        bA = next_bank(0, 4)
        bB = next_bank(0, 4)
        for (bank, jb) in ((bA, 0), (bB, 1)):
            for kc in range(KC):
                P.mm(pb[bank][:], wt[:, kc, jb * 128:(jb + 1) * 128], hT[:, kc, tg * 512:(tg + 1) * 512],
                     kc == 0, kc == KC - 1, [wb] + b_hT[tg * 4:tg * 4 + 4], [pbb[bank]], signal=(kc == KC - 1))
        ts_ = slice(tg * 512, (tg + 1) * 512)
        P.tt("dve", rt1[:], pb[bA][:], tabA[:, 0, ts_], ALU.mult, [pbb[bA], b_tabA], [b_rt1])
        P.tt("dve", rsb[:], pb[bB][:], tabA[:, 1, ts_], ALU.mult, [pbb[bB], b_tabA], [b_rsb])
        P.tt("dve", krT[:, ts_], rt1[:], rsb[:], ALU.add, [b_rt1, b_rsb], [b_krT])

    dump("cqn", cqn[:, 0, :], [b_cqn])
    dump("ckvn", ckvn[:, 3, :], [b_ckvn])
    dump("krT", krT[:], [b_krT])
    if stop_after == "A3":
        P.emit(final_bufs)
        return nc

    o = O_HT
    tab, b_tab = P.alloc("tab", [128, 2, S], F32, o); o += 16 * KB
    P.copy("act", tab[:, 0, :], tabA[:, 0, :], [b_tabA], [b_tab])
    P.copy("act", tab[:, 1, :], tabA[:, 1, :], [b_tabA], [b_tab])
    hb = []
    for i in range(2):
        d = {}
        d["qn"], d["b_qn"] = P.alloc("qn%d" % i, [128, S], BF16, o); o += 4 * KB
        d["qr"], d["b_qr"] = P.alloc("qr%d" % i, [128, S], BF16, o); o += 4 * KB
        d["kn"], d["b_kn"] = P.alloc("kn%d" % i, [128, S], BF16, o); o += 4 * KB
        d["v"], d["b_v"] = P.alloc("v%d" % i, [128, NT, 128], BF16, o); o += 4 * KB
        hb.append(d)
    PT = []
    for i in range(4):
        t, b = P.alloc("PT%d" % i, [128, 512], BF16, o); o += KB
        PT.append((t, b))
    rec, b_rec = P.alloc("rec", [128, 512], F32, o); o += 2 * KB
    ra, b_ra = P.alloc("ra", [128, 512], F32, o); o += 2 * KB
    rb, b_rb = P.alloc("rb", [128, 512], F32, o); o += 2 * KB
    assert o <= O_HT + 64 * KB
    aoutT, b_aoutT = P.alloc("aoutT", [128, 8, S], BF16, O_AO)
    wq = []
    for i in range(2):
        t, b = P.alloc("wq%d" % i, [128, 6, 384], BF16, O_R4 + i * 8 * KB, dma=True)
        t2, b2 = P.alloc("wkv%d" % i, [128, 4, 256], BF16, O_R4 + i * 8 * KB + 4608, dma=True)
        wq.append((t, b, t2, b2))

    ptc = 0
    for h in range(8):
        d = hb[h % 2]
        wqt, wqb, wkt, wkb = wq[h % 2]
        P.dma("pool", wqt[:], wuq_d[:, h * 384:(h + 1) * 384].rearrange("(kc p) c -> p kc c", p=128),
              writes=[wqb], sem=wqb.sem)
        P.dma("pool", wkt[:, :, 0:128], wkn_d[:, h * 128:(h + 1) * 128].rearrange("(kc p) c -> p kc c", p=128),
              writes=[wkb], sem=wkb.sem)
        P.dma("pool", wkt[:, :, 128:256], wv_d[:, h * 128:(h + 1) * 128].rearrange("(kc p) c -> p kc c", p=128),
              writes=[wkb], sem=wkb.sem)
        conv_some(8)
        for tg in range(4):
            ts_ = slice(tg * 512, (tg + 1) * 512)
            bank = next_bank(0, 4)
            for kc in range(6):
                P.mm(pb[bank][:], wqt[:, kc, 0:128], cqn[:, kc, ts_], kc == 0, kc == 5, [wqb, b_cqn], [pbb[bank]],
                     signal=(kc == 5))
            P.copy("act", d["qn"][:, ts_], pb[bank][:], [pbb[bank]], [d["b_qn"]])
            bA = next_bank(0, 4)
            bB = next_bank(0, 4)
            for (bank, c0) in ((bA, 128), (bB, 256)):
                for kc in range(6):
                    P.mm(pb[bank][:], wqt[:, kc, c0:c0 + 128], cqn[:, kc, ts_], kc == 0, kc == 5, [wqb, b_cqn],
                         [pbb[bank]], signal=(kc == 5))
            P.tt("dve", ra[:], pb[bA][:], tab[:, 0, ts_], ALU.mult, [pbb[bA], b_tab], [b_ra])
            P.tt("dve", rb[:], pb[bB][:], tab[:, 1, ts_], ALU.mult, [pbb[bB], b_tab], [b_rb])
            P.tt("dve", d["qr"][:, ts_], ra[:], rb[:], ALU.add, [b_ra, b_rb], [d["b_qr"]])
            bank = next_bank(0, 4)
            for kc in range(4):
                P.mm(pb[bank][:], wkt[:, kc, 0:128], ckvn[:, kc, ts_], kc == 0, kc == 3, [wkb, b_ckvn], [pbb[bank]],
                     signal=(kc == 3))
            P.copy("act", d["kn"][:, ts_], pb[bank][:], [pbb[bank]], [d["b_kn"]])
            bank = next_bank(0, 4)
            for i in range(4):
                tt = tg * 4 + i
                for kc in range(4):
                    P.mm(pb[bank][:, i * 128:(i + 1) * 128], ckvn[:, kc, tt * 128:(tt + 1) * 128], wkt[:, kc, 128:256],
                         kc == 0, kc == 3, [wkb, b_ckvn], [pbb[bank]], signal=(kc == 3 and i == 3))
            P.copy("act", d["v"][:, tg * 4:(tg + 1) * 4, :], pb[bank][:].rearrange("p (a b) -> p a b", a=4),
                   [pbb[bank]], [d["b_v"]])
        for qg in range(4):
            bo = 4 + (qg % 2) * 2
            bs = bo + 1
            nkb = 4 * (qg + 1)
            def s_mm(kb):
                j = kb - 4 * qg
                q0 = 0 if j < 0 else 128 * j
                n = 512 - q0
                qs = slice(qg * 512 + q0, (qg + 1) * 512)
                ks = slice(kb * 128, (kb + 1) * 128)
                bank = next_bank(0, 4)
                P.mm(pb[bank][:, 0:n], d["kn"][:, ks], d["qn"][:, qs], True, False, [d["b_kn"], d["b_qn"]],
                     [pbb[bank]], signal=False)
                P.mm(pb[bank][:, 0:n], krT[:, ks], d["qr"][:, qs], False, True, [b_krT, d["b_qr"]],
                     [pbb[bank]], signal=True)
                return (bank, j, q0, n)

            nxt = s_mm(0)
            for kb in range(nkb):
                bank, j, q0, n = nxt
                if kb + 1 < nkb:
                    nxt = s_mm(kb + 1)
                pt_, ptb = PT[ptc % 4]
                ptc += 1
                P.act(pt_[:, 0:n], pb[bank][:, 0:n], AF.Exp, [pbb[bank]], [ptb], scale=SCALE)
                if j >= 0:
                    P.memset("dve", pt_[64:128, 0:64], 0.0, [ptb])
                P.mm(pb[bo][:, q0:512], d["v"][:, kb, :], pt_[:, 0:n], kb == 0, kb == nkb - 1, [d["b_v"], ptb],
                     [pbb[bo]], signal=False)
                P.mm(pb[bs][:, q0:512], ones_b, pt_[:, 0:n], kb == 0, kb == nkb - 1, [b_mbf, ptb],
                     [pbb[bs]], signal=True)
            P.recip(rec[:], pb[bs][:], [pbb[bs]], [b_rec])
            P.tt("dve", aoutT[:, h, qg * 512:(qg + 1) * 512], pb[bo][:], rec[:], ALU.mult, [pbb[bo], b_rec], [b_aoutT])

    dump("aoutT", aoutT[:, 0, :], [b_aoutT])
    dump("aoutT7", aoutT[:, 7, :], [b_aoutT])
    if stop_after == "A4":
        P.emit(final_bufs)
        return nc

    O_X1 = O_HT
    O_H2 = O_X1 + 32 * KB
    O_WG = O_H2 + 16 * KB
    assert O_WG + 16 * KB <= O_XS
    O_ACT = O_AO + 32 * KB
    O_WD = O_ACT + 44 * KB
    O_GT = O_WD + 8 * KB
    assert O_GT + 2080 + 2048 <= SB_END
    x1, b_x1_all = P.alloc("x1", [128, 4, D], F32, O_X1)
    h2T, b_h2T0 = P.alloc("h2T", [128, KC, 512], BF16, O_H2)
    actT, b_actT_all = P.alloc("actT", [128, NJ, 512], BF16, O_ACT)
    NWO = 3
    wos0 = []
    for i in range(NWO):
        t, b = P.alloc("wos%d" % i, [128, KC, 256], BF16, O_ACT + i * 8 * KB, dma=True)
        wos0.append((t, b))
    xn2b, b_xs2_0 = P.alloc("xn2b", [128, D], BF16, O_ACT + 40 * KB)
    xn2, b_xn2_0 = P.alloc("xn2", [128, D], BF16, O_ACT + 24 * KB)
    gbc, b_gbc_0 = P.alloc("gbc", [128, D], F32, O_ACT + 28 * KB, dma=True)
    junk3, b_junk3_0 = P.alloc("junk3", [128, D], BF16, O_ACT + 36 * KB)
    O_SP = O_GT + 2080 + 2048 + 192
    assert O_SP + 8 * KB <= SB_END
    NWG = 3
    wgs0 = []
    for i in range(NWG):
        off = O_WG + i * 8 * KB if i < 2 else O_SP
        t, b = P.alloc("wgs%d" % i, [128, KC, 128], BF16, off, dma=True)
        t2, b2 = P.alloc("wus%d" % i, [128, KC, 128], BF16, off + 4 * KB, dma=True)
        wgs0.append((t, b, t2, b2))
    NWD = 4
    wds = []
    for i in range(NWD):
        t, b = P.alloc("wds%d" % i, [128, 2, 512], BF16, O_WD + i * 2 * KB, dma=True)
        wds.append((t, b))
    conv_some(len(conv_jobs))
    ft0 = []
    for i in range(4):
        t, b = P.alloc("ft%d" % i, [128, D], F32, O_H2 + i * 8 * KB, dma=True)
        ft0.append((t, b))
    gt, b_gt = P.alloc("gt", [128, 2 + 512], F32, O_GT)
    gacc, b_gacc = P.alloc("gacc", [128, 512], F32, O_GT + 2080)
    ssqp, b_ssqp = P.alloc("ssqp", [128, 48], F32, O_GT + 2080 + 2048)
    st2, b_st2 = stat, b_stat
    P.memset("dve", halo[:], 0.0, [b_halo])
    mixT = lambda kc: (aoutT[:, kc, :] if kc < 8 else boutT[:, kc - 8, :])
    mix_bufs = [b_aoutT] + b_boutT
    woc = 0
    wgc = 0
    wdc = 0
    ngroups = 4 if stop_after is None else 1
    for g in range(ngroups):
        b_x1 = [P.newbuf("x1_%d_%d" % (g, i), b_x1_all.rng, sem="x1acc%d" % i) for i in range(4)]
        wos = [(t, P.rebuf(b)) for (t, b) in wos0]
        xn2_l = [(xn2, P.rebuf(b_xn2_0)), (xn2b, P.rebuf(b_xs2_0))]
        b_gbc = P.rebuf(b_gbc_0)
        b_junk3 = P.rebuf(b_junk3_0)
        b_h2T = P.rebuf(b_h2T0)
        for cg in range(8):
            wt, wb = wos[woc % NWO]
            woc += 1
            P.dma("sp", wt[:], wout_s[cg].rearrange("p (kc c) -> p kc c", kc=KC), reads=[b_scr],
                  writes=[wb], sem=wb.sem)
            for i in range(4):
                tt = g * 4 + i
                bank = next_bank(0, 4)
                for kc in range(KC):
                    P.mm(pb[bank][:, 0:256], mixT(kc)[:, tt * 128:(tt + 1) * 128], wt[:, kc, :], kc == 0, kc == KC - 1,
                         [wb] + mix_bufs, [pbb[bank]], signal=(kc == KC - 1))
                P.copy("act", x1[:, i, cg * 256:(cg + 1) * 256], pb[bank][:, 0:256], [pbb[bank]], [b_x1[i]])
                P.act(junk3[:, 0:256], pb[bank][:, 0:256], AF.Square, [pbb[bank]], [b_junk3, b_ssqp],
                      accum_out=ssqp[:, i * 8 + cg:i * 8 + cg + 1])
        P.dma("pool", gbc[:], gpost_d[0], writes=[b_gbc], sem=b_gbc.sem)
        for i in range(4):
            P.op("dve", lambda e, i=i: e.reduce_sum(out=st2[:, 16 + i:17 + i], in_=ssqp[:, i * 8:i * 8 + 8],
                                                     axis=mybir.AxisListType.X), [b_ssqp], [b_st2])
        rstd_from_ssq("dve", st2[:, 20:24], st2[:, 16:20], D, [b_st2], [b_st2])
        for i in range(4):
            tt = g * 4 + i
            P.stt("dve", x1[:, i, :], x1[:, i, :], st2[:, 20 + i:21 + i], gbc[:], ALU.mult, ALU.mult,
                  [b_x1[i], b_st2, b_gbc], [b_x1[i]])
            P.dma("pool", x1[:, i, :], x_d[tt * 128:(tt + 1) * 128, :], writes=[b_x1[i]], sem=b_x1[i].sem,
                  accum=ALU.add)
        for i in range(4):
            P.act(junk3[:], x1[:, i, :], AF.Square, [b_x1[i]], [b_junk3, b_st2], accum_out=st2[:, 24 + i:25 + i])
        rstd_from_ssq("dve", st2[:, 28:32], st2[:, 24:28], D, [b_st2], [b_st2])
        for i in range(4):
            xn2_t, b_xn2_i = xn2_l[i % 2]
            P.act(xn2_t[:], x1[:, i, :], AF.Copy, [b_x1[i], b_st2], [b_xn2_i], scale=st2[:, 28 + i:29 + i])
            for q in range(4):
                bank = next_bank(0, 4)
                pt = pb[bank][:].bitcast(BF16)
                for ii in range(4):
                    kc = q * 4 + ii
                    P.tr(pt[:, ii * 128:(ii + 1) * 128], xn2_t[:, kc * 128:(kc + 1) * 128], ident_b,
                         [b_xn2_i, b_mbf], [pbb[bank]], signal=(ii == 3))
                P.tt("dve", h2T[:, q * 4:(q + 1) * 4, i * 128:(i + 1) * 128],
                     pt[:, 0:512].rearrange("p (a b) -> p a b", a=4),
                     bc(cvec[:, C_GF + q * 4:C_GF + q * 4 + 4].unsqueeze(2), [128, 4, 128]),
                     ALU.mult, [pbb[bank], b_cvec], [b_h2T])
        b_outd = []
        for i in range(4):
            tt = g * 4 + i
            bo_ = P.newbuf("outd_%d_%d" % (g, i), sem="outd%d" % i)
            b_outd.append(bo_)
            P.dma("pool", out_d[tt * 128:(tt + 1) * 128, :], x1[:, i, :], reads=[b_x1[i]], writes=[bo_], sem=bo_.sem)
            final_bufs.append(bo_)
        if debug and g == 0:
            dump("x1", x1[:, 0, :], b_x1)
            dump("h2T", h2T[:, 0, :], [b_h2T])
        b_actT = [P.sub(b_actT_all, "actT_%d_%d" % (g, j)) for j in range(NJ)]
        wgs = [(t, P.rebuf(b), t2, P.rebuf(b2)) for (t, b, t2, b2) in wgs0]
        for j in range(NJ):
            wgt, wgb, wut, wub = wgs[wgc % NWG]
            wgc += 1
            P.dma("sp", wgt[:], wgate_s[j].rearrange("p (kc c) -> p kc c", kc=KC), reads=[b_scr],
                  writes=[wgb], sem=wgb.sem)
            P.dma("sp", wut[:], wup_s[j].rearrange("p (kc c) -> p kc c", kc=KC), reads=[b_scr],
                  writes=[wub], sem=wub.sem)
            bg = next_bank(0, 4)
            bu = next_bank(0, 4)
            for kc in range(KC):
                P.mm(pb[bg][:], wgt[:, kc, :], h2T[:, kc, :], kc == 0, kc == KC - 1, [wgb, b_h2T], [pbb[bg]],
                     signal=(kc == KC - 1))
            for kc in range(KC):
                P.mm(pb[bu][:], wut[:, kc, :], h2T[:, kc, :], kc == 0, kc == KC - 1, [wub, b_h2T], [pbb[bu]],
                     signal=(kc == KC - 1))
            P.copy("dve", gt[:, 0:2], halo[:, j, :], [b_halo], [b_gt])
            P.copy("act", gt[:, 2:514], pb[bg][:], [pbb[bg]], [b_gt])
            P.copy("dve", halo[:, j, :], gt[:, 512:514], [b_gt], [b_halo])
            fw = C_FW + j * 3
            P.ts("dve", gacc[:], gt[:, 2:514], cvec[:, fw + 2:fw + 3], cvec[:, C_FB + j:C_FB + j + 1],
                 ALU.mult, ALU.add, [b_gt, b_cvec], [b_gacc])
            for k in range(2):
                P.stt("dve", gacc[:], gt[:, k:k + 512], cvec[:, fw + k:fw + k + 1], gacc[:], ALU.mult, ALU.add,
                      [b_gt, b_gacc, b_cvec], [b_gacc])
            P.act(gacc[:], gacc[:], AF.Gelu_apprx_tanh, [b_gacc], [b_gacc])
            P.tt("dve", actT[:, j, :], gacc[:], pb[bu][:], ALU.mult, [b_gacc, pbb[bu]], [b_actT[j]])
        if debug and g == 0:
            dump("actT", actT[:, 0, :], b_actT)
        ft = [(t, P.rebuf(b)) for (t, b) in ft0]
        for cg in range(4):
            for kh in range(NJ // 2):
                wt, wb = wds[wdc % NWD]
                wdc += 1
                P.dma("sp", wt[:], wdown_s[cg * 11 + kh // 2][:, (kh % 2) * 1024:(kh % 2) * 1024 + 1024]
                      .rearrange("p (kc c) -> p kc c", kc=2), reads=[b_scr], writes=[wb], sem=wb.sem)
                for kk in range(2):
                    k = kh * 2 + kk
                    for i in range(4):
                        P.mm(pb[4 + i][:], actT[:, k, i * 128:(i + 1) * 128], wt[:, kk, :], k == 0, k == NJ - 1,
                             [wb, b_actT[k]], [pbb[4 + i]], signal=(k == NJ - 1) or (kk == 1 and i == 3))
            for i in range(4):
                fti, ftb = ft[i]
                P.copy("act", fti[:, cg * 512:(cg + 1) * 512], pb[4 + i][:], [pbb[4 + i]], [ftb])
                P.act(gacc[:], pb[4 + i][:], AF.Square, [pbb[4 + i]], [b_gacc, b_ssqp],
                      accum_out=ssqp[:, 32 + i * 4 + cg:32 + i * 4 + cg + 1])
        b_gbc = P.rebuf(b_gbc_0)
        b_junk3 = P.rebuf(b_junk3_0)
        P.dma("pool", gbc[:], gpost_d[1], writes=[b_gbc], sem=b_gbc.sem)
        for i in range(4):
            P.op("dve", lambda e, i=i: e.reduce_sum(out=st2[:, 32 + i:33 + i], in_=ssqp[:, 32 + i * 4:32 + i * 4 + 4],
                                                     axis=mybir.AxisListType.X), [b_ssqp], [b_st2])
        rstd_from_ssq("dve", st2[:, 36:40], st2[:, 32:36], D, [b_st2], [b_st2])
        for i in range(4):
            tt = g * 4 + i
            fti, ftb = ft[i]
            P.stt("dve", fti[:], fti[:], st2[:, 36 + i:37 + i], gbc[:], ALU.mult, ALU.mult, [ftb, b_st2, b_gbc], [ftb])
            P.dma("pool", out_d[tt * 128:(tt + 1) * 128, :], fti[:], reads=[ftb], writes=[b_outd[i]],
                  sem=b_outd[i].sem, accum=ALU.add)
    P.emit(final_bufs)
    return nc


def host_layout(inputs):
    f = np.float32
    w_in = np.asarray(inputs["w_in"][0], f)
    cq = w_in[:, 0:768]; ckv = w_in[:, 768:1280]; kr = w_in[:, 1280:1344]
    z = w_in[:, 1344:2368]; xbc = w_in[:, 2368:3904]; dt = w_in[:, 3904:3920]
    z64 = np.zeros((D, 64), f)
    kr_sw = np.concatenate([kr[:, 32:64], kr[:, 0:32]], axis=1)
    w_in_p = np.ascontiguousarray(np.concatenate([cq, ckv, kr, z64, kr_sw, z64, xbc, z, dt], axis=1))
    assert w_in_p.shape[1] == WIN_COLS
    w_uq = np.asarray(inputs["w_uq"][0], f)
    zq = np.zeros((768, 64), f)
    parts = []
    for h in range(8):
        nope = w_uq[:, h * 192:h * 192 + 128]
        rope = w_uq[:, h * 192 + 128:h * 192 + 192]
        rsw = np.concatenate([rope[:, 32:64], rope[:, 0:32]], axis=1)
        parts += [nope, rope, zq, rsw, zq]
    w_uq_p = np.ascontiguousarray(np.concatenate(parts, axis=1))
    w_ukv = np.asarray(inputs["w_ukv"][0], f)
    w_kn_p = np.ascontiguousarray(np.concatenate([w_ukv[:, h * 256:h * 256 + 128] for h in range(8)], axis=1))
    w_v_p = np.ascontiguousarray(np.concatenate([w_ukv[:, h * 256 + 128:h * 256 + 256] for h in range(8)], axis=1))

    def pm(v, n):
        return np.asarray(v, f).reshape(n, 128).T

    cw = np.asarray(inputs["ssm_conv_w"][0], f)
    fw = np.asarray(inputs["ffn_conv_w"][0], f)
    cvec = np.concatenate([
        pm(inputs["mix_pre_g"][0], 16), pm(inputs["q_norm_g"][0], 6), pm(inputs["kv_norm_g"][0], 4),
        cw.reshape(4, 12, 128).transpose(2, 1, 0).reshape(128, 48),
        pm(inputs["ssm_conv_b"][0], 12), pm(inputs["ffn_pre_g"][0], 16),
        fw.reshape(3, NJ, 128).transpose(2, 1, 0).reshape(128, NJ * 3),
        pm(inputs["ffn_conv_b"][0], NJ)], axis=1)
    cvec = np.ascontiguousarray(cvec, f)
    assert cvec.shape == (128, 278)
    row = np.concatenate([np.asarray(inputs["dt_bias"][0], f), np.asarray(inputs["a_log"][0], f),
                          np.asarray(inputs["d_skip"][0], f), np.asarray(inputs["ssm_norm_g"][0], f)])
    cbc = np.ascontiguousarray(np.broadcast_to(row[None, :], (128, row.size)), f)
    gpost = np.ascontiguousarray(np.stack([
        np.broadcast_to(np.asarray(inputs["mix_post_g"][0], f)[None, :], (128, D)),
        np.broadcast_to(np.asarray(inputs["ffn_post_g"][0], f)[None, :], (128, D))]), f)
    inv = 1.0 / (10000.0 ** (np.arange(0, 64, 2, dtype=np.float32) / 64.0))
    ang = np.arange(S, dtype=np.float32)[:, None] * inv[None, :].astype(np.float32)
    cos = np.cos(ang).T.astype(f); sin = np.sin(ang).T.astype(f)
    tab = np.zeros((2, 128, S), f)
    tab[0, 0:32] = cos; tab[0, 32:64] = cos
    tab[1, 0:32] = -sin; tab[1, 32:64] = sin
    k = np.arange(128)
    masks = np.zeros((128, 4, 128), f)
    masks[:, 0, :] = np.eye(128)
    masks[:, 1, :] = 1.0
    masks[:, 2, :] = (k[:, None] > k[None, :])
    masks[:, 3, :] = (k[:, None] <= k[None, :])
    shared = {
        "w_in_p": w_in_p, "w_uq_p": w_uq_p, "w_kn_p": w_kn_p, "w_v_p": w_v_p,
        "w_out_b": np.ascontiguousarray(np.asarray(inputs["w_out"][0], f).reshape(16, 128, 8, 256)
                                        .transpose(2, 1, 0, 3).reshape(8, 128, 4096)),
        "w_gate_b": np.ascontiguousarray(np.asarray(inputs["w_gate"][0], f).reshape(16, 128, NJ, 128)
                                         .transpose(2, 1, 0, 3).reshape(NJ, 128, 2048)),
        "w_up_b": np.ascontiguousarray(np.asarray(inputs["w_up"][0], f).reshape(16, 128, NJ, 128)
                                       .transpose(2, 1, 0, 3).reshape(NJ, 128, 2048)),
        "w_down_b": np.ascontiguousarray(np.asarray(inputs["w_down"][0], f).reshape(11, 4, 128, 4, 512)
                                         .transpose(3, 0, 2, 1, 4).reshape(44, 128, 2048)),
        "cvec": cvec, "cbc": cbc, "gpost": gpost, "ropetab": tab, "masks": masks,
    }
    return shared


def kernel(**inputs):
    x = np.asarray(inputs["x"], np.float32)
    shared = host_layout(inputs)
    nc = build_program()
    in_maps = []
    for c in range(8):
        m = dict(shared)
        m["x"] = np.ascontiguousarray(x[c])
        in_maps.append(m)
    res = run_bass_kernel_spmd(nc, in_maps, core_ids=list(range(8)))
    return np.stack([np.asarray(r["out"], np.float32) for r in res.results], axis=0)
```
